# Optimizing a Trainium2 kernel written in Bass

```python
import math
import jax, jax.numpy as jnp
from jax import lax
import numpy as np

D_MODEL = 1024
BATCH = 8
SEQ = 4096
DEPTH = 1

CHUNK = 64
LEFT_CHUNKS = 8
BAND = (LEFT_CHUNKS + 1) * CHUNK
HEAD_DIM = 64
A_HEADS = 8
A_WIDTH = A_HEADS * HEAD_DIM
REL_CLIP = 256
B_HEADS = 4
B_QK_WIDTH = B_HEADS * 2 * HEAD_DIM
B_V_WIDTH = B_HEADS * 2 * HEAD_DIM
ROPE_THETA = 500000.0
ROT_DIM = HEAD_DIM // 4
D_FF = 2816
Q_BLOCK = 128
EPS = 1e-6
NEG = -1e30
IN_SIZES = (A_WIDTH, A_WIDTH, A_WIDTH, B_QK_WIDTH, B_QK_WIDTH, B_V_WIDTH, D_MODEL, D_MODEL)
IN_WIDTH = sum(IN_SIZES)
IN_SPLITS = np.cumsum(IN_SIZES)[:-1].tolist()

kernel_name = 'hybrid_chunk_diff_macaron'


def rms_norm(x, g):
    xf = x.astype(jnp.float32)
    y = xf * lax.rsqrt(jnp.mean(xf * xf, axis=-1, keepdims=True) + EPS)
    return (y * g.astype(jnp.float32)).astype(x.dtype)


def swiglu(x, w_gu, w_down):
    g, u = jnp.split(x @ w_gu, 2, axis=-1)
    return (jax.nn.silu(g) * u) @ w_down


def partial_rotary(x):
    s = x.shape[1]
    pos = jnp.arange(s, dtype=jnp.float32)
    inv = ROPE_THETA ** (-jnp.arange(0, ROT_DIM, 2, dtype=jnp.float32) / ROT_DIM)
    ang = pos[:, None] * inv[None, :]
    cos = jnp.concatenate([jnp.cos(ang)] * 2, axis=-1)[None, :, None, :]
    sin = jnp.concatenate([jnp.sin(ang)] * 2, axis=-1)[None, :, None, :]
    xr = x[..., :ROT_DIM].astype(jnp.float32)
    x1, x2 = xr[..., :ROT_DIM // 2], xr[..., ROT_DIM // 2:]
    rot = jnp.concatenate([-x2, x1], axis=-1)
    xr = (xr * cos + rot * sin).astype(x.dtype)
    return jnp.concatenate([xr, x[..., ROT_DIM:]], axis=-1)


def chunk_band_attention(q, k, v, rel_table):
    b, s, h, d = q.shape
    nc = s // CHUNK
    pad = LEFT_CHUNKS * CHUNK
    kp = jnp.pad(k, ((0, 0), (pad, 0), (0, 0), (0, 0)))
    vp = jnp.pad(v, ((0, 0), (pad, 0), (0, 0), (0, 0)))
    i = jnp.arange(CHUNK)[:, None]
    j = jnp.arange(BAND)[None, :]
    rel = i - j + pad
    idx = jnp.clip(rel, -REL_CLIP, REL_CLIP) + REL_CLIP
    bias = jnp.transpose(rel_table[idx], (2, 0, 1)).astype(jnp.float32)
    scale = 1.0 / math.sqrt(d)

    def one_chunk(c):
        qc = lax.dynamic_slice_in_dim(q, c * CHUNK, CHUNK, axis=1)
        kb = lax.dynamic_slice_in_dim(kp, c * CHUNK, BAND, axis=1)
        vb = lax.dynamic_slice_in_dim(vp, c * CHUNK, BAND, axis=1)
        sc = jnp.einsum('bqhd,bkhd->bhqk', qc, kb).astype(jnp.float32) * scale + bias
        valid = (j >= pad - c * CHUNK)[None, None]
        p = jax.nn.softmax(jnp.where(valid, sc, NEG), axis=-1).astype(v.dtype)
        return jnp.einsum('bhqk,bkhd->bqhd', p, vb)

    out = lax.map(one_chunk, jnp.arange(nc))
    return jnp.transpose(out, (1, 0, 2, 3, 4)).reshape(b, s, h * d)


def diff_attention(q1, q2, k1, k2, v, lam, g_sub, lambda_init):
    b, s, h, d = q1.shape
    nb = s // Q_BLOCK
    kchunk = jnp.arange(s) // CHUNK
    scale = 1.0 / math.sqrt(d)

    def one_block(blk):
        st = blk * Q_BLOCK
        q1b = lax.dynamic_slice_in_dim(q1, st, Q_BLOCK, axis=1)
        q2b = lax.dynamic_slice_in_dim(q2, st, Q_BLOCK, axis=1)
        qchunk = (st + jnp.arange(Q_BLOCK)) // CHUNK
        mask = (kchunk[None, :] <= qchunk[:, None])[None, None]
        s1 = jnp.einsum('bqhd,bkhd->bhqk', q1b, k1).astype(jnp.float32) * scale
        s2 = jnp.einsum('bqhd,bkhd->bhqk', q2b, k2).astype(jnp.float32) * scale
        p1 = jax.nn.softmax(jnp.where(mask, s1, NEG), axis=-1)
        p2 = jax.nn.softmax(jnp.where(mask, s2, NEG), axis=-1)
        a = (p1 - lam * p2).astype(v.dtype)
        o = jnp.einsum('bhqk,bkhe->bqhe', a, v)
        return rms_norm(o, g_sub) * (1.0 - lambda_init)

    out = lax.map(one_block, jnp.arange(nb))
    return jnp.transpose(out, (1, 0, 2, 3, 4)).reshape(b, s, h * 2 * d)


def setup_inputs(seed: int = 0) -> dict:
    key = jax.random.key(seed)
    ks = jax.random.split(key, 24)
    f = jnp.float32

    def w(k, shape, fan_in):
        return jax.random.normal(k, shape, f) * fan_in ** -0.5

    def gain(k, n):
        return 1.0 + 0.01 * jax.random.normal(k, (DEPTH, n), f)

    return {
        'x': jax.random.normal(ks[0], (BATCH, SEQ, D_MODEL), f),
        'g_ffn1': gain(ks[1], D_MODEL),
        'w_ffn1_gu': w(ks[2], (DEPTH, D_MODEL, 2 * D_FF), D_MODEL),
        'w_ffn1_down': w(ks[3], (DEPTH, D_FF, D_MODEL), D_FF),
        'g_mix': gain(ks[4], D_MODEL),
        'w_in': w(ks[5], (DEPTH, D_MODEL, IN_WIDTH), D_MODEL),
        'qn_a': gain(ks[6], HEAD_DIM),
        'kn_a': gain(ks[7], HEAD_DIM),
        'rel_bias': 0.1 * jax.random.normal(ks[8], (DEPTH, 2 * REL_CLIP + 1, A_HEADS), f),
        'qn_b': gain(ks[9], HEAD_DIM),
        'kn_b': gain(ks[10], HEAD_DIM),
        'lambda_q1': 0.1 * jax.random.normal(ks[11], (DEPTH, HEAD_DIM), f),
        'lambda_k1': 0.1 * jax.random.normal(ks[12], (DEPTH, HEAD_DIM), f),
        'lambda_q2': 0.1 * jax.random.normal(ks[13], (DEPTH, HEAD_DIM), f),
        'lambda_k2': 0.1 * jax.random.normal(ks[14], (DEPTH, HEAD_DIM), f),
        'g_subln': gain(ks[15], 2 * HEAD_DIM),
        'w_up_a': w(ks[16], (DEPTH, A_WIDTH, D_MODEL), A_WIDTH),
        'w_up_b': w(ks[17], (DEPTH, B_V_WIDTH, D_MODEL), B_V_WIDTH),
        'w_out': w(ks[18], (DEPTH, D_MODEL, D_MODEL), D_MODEL),
        'g_ffn2': gain(ks[19], D_MODEL),
        'w_ffn2_gu': w(ks[20], (DEPTH, D_MODEL, 2 * D_FF), D_MODEL),
        'w_ffn2_down': w(ks[21], (DEPTH, D_FF, D_MODEL), D_FF),
        'g_final': gain(ks[22], D_MODEL),
    }


def reference(x, g_ffn1, w_ffn1_gu, w_ffn1_down, g_mix, w_in, qn_a, kn_a, rel_bias,
              qn_b, kn_b, lambda_q1, lambda_k1, lambda_q2, lambda_k2, g_subln,
              w_up_a, w_up_b, w_out, g_ffn2, w_ffn2_gu, w_ffn2_down, g_final):
    b, s, _ = x.shape
    for l in range(DEPTH):
        x = x + 0.5 * swiglu(rms_norm(x, g_ffn1[l]), w_ffn1_gu[l], w_ffn1_down[l])

        h = rms_norm(x, g_mix[l])
        qa, ka, va, qb, kb, vb, ga, gb = jnp.split(h @ w_in[l], IN_SPLITS, axis=-1)

        qa = rms_norm(qa.reshape(b, s, A_HEADS, HEAD_DIM), qn_a[l])
        ka = rms_norm(ka.reshape(b, s, A_HEADS, HEAD_DIM), kn_a[l])
        va = va.reshape(b, s, A_HEADS, HEAD_DIM)
        oa = chunk_band_attention(qa, ka, va, rel_bias[l])

        qb = partial_rotary(rms_norm(qb.reshape(b, s, 2 * B_HEADS, HEAD_DIM), qn_b[l]))
        kb = partial_rotary(rms_norm(kb.reshape(b, s, 2 * B_HEADS, HEAD_DIM), kn_b[l]))
        vb = vb.reshape(b, s, B_HEADS, 2 * HEAD_DIM)
        lambda_init = 0.8 - 0.6 * math.exp(-0.3 * l)
        lam = (jnp.exp(jnp.sum(lambda_q1[l].astype(jnp.float32) * lambda_k1[l].astype(jnp.float32)))
               - jnp.exp(jnp.sum(lambda_q2[l].astype(jnp.float32) * lambda_k2[l].astype(jnp.float32)))
               + lambda_init)
        ob = diff_attention(qb[:, :, 0::2], qb[:, :, 1::2], kb[:, :, 0::2], kb[:, :, 1::2],
                            vb, lam, g_subln[l], lambda_init)

        y = jax.nn.sigmoid(ga) * (oa @ w_up_a[l]) + jax.nn.sigmoid(gb) * (ob @ w_up_b[l])
        x = x + y @ w_out[l]

        x = x + 0.5 * swiglu(rms_norm(x, g_ffn2[l]), w_ffn2_gu[l], w_ffn2_down[l])
        x = rms_norm(x, g_final[l])
    return x
```

```python
import math
import numpy as np
import concourse.bass as bass
import concourse.mybir as mybir
from concourse.bass_utils import run_bass_kernel_spmd

F32 = mybir.dt.float32
BF16 = mybir.dt.bfloat16
AF = mybir.ActivationFunctionType
ALU = mybir.AluOpType
AX = mybir.AxisListType

SEQ = 4096
D = 1024
FF = 2816
NKC = 8
NFC = 22
NT = 512
NS = 4
NTILES = SEQ // NT
NW = 4
NSLOT = 96
EPS = 1e-6
NEG = -30000.0
LAMBDA_INIT = 0.8 - 0.6 * math.exp(-0.3 * 0)
CW = 640


class Buf:
    __slots__ = ("name", "w", "r")

    def __init__(self, name):
        self.name = name
        self.w = None
        self.r = {}


class Eng:
    def __init__(self, name, eng, sem, self_raw):
        self.name = name
        self.eng = eng
        self.sem = sem
        self.count = 0
        self.waited = {}
        self.self_raw = self_raw


class Sched:
    def __init__(self, nc):
        self.nc = nc
        self.marks = []
        self.E = {}
        for name, eng, sr in (("pe", nc.tensor, False), ("act", nc.scalar, True), ("dve", nc.vector, True),
                              ("pool", nc.gpsimd, True), ("sp", nc.sync, False)):
            sem = nc.semaphore("sem_" + name).__enter__()
            self.E[name] = Eng(name, eng, sem, sr)

    def _wait(self, E, tok, raw):
        if tok[0] == "e":
            src = self.E[tok[1]]
            val = tok[2]
            if src is E and not E.self_raw:
                return
            key = ("e", src.name)
            if E.waited.get(key, 0) >= val:
                return
            E.eng.wait_ge(src.sem, val)
            E.waited[key] = val
        else:
            sem = tok[1]
            val = tok[2]
            key = ("d", id(sem))
            if E.waited.get(key, 0) >= val:
                return
            E.eng.wait_ge(sem, val)
            E.waited[key] = val

    def _deps(self, E, reads, writes):
        for b in reads:
            if b.w is not None:
                self._wait(E, b.w, True)
        for b in writes:
            if b.w is not None:
                self._wait(E, b.w, False)
            for tok in b.r.values():
                self._wait(E, tok, False)

    def op(self, ename, fn, reads=(), writes=(), inc=True):
        E = self.E[ename]
        self._deps(E, reads, writes)
        inst = fn()
        E.nops = getattr(E, "nops", 0) + 1
        if inc:
            E.count += 1
            inst.then_inc(E.sem, 1)
            tok = ("e", ename, E.count)
        else:
            tok = ("e", ename, E.count + 1)
        for b in reads:
            b.r[ename] = tok
        for b in writes:
            b.w = tok
            b.r = {}
        return inst

    def dma(self, qname, out, in_, sem, semcnt, reads=(), writes=()):
        E = self.E[qname]
        self._deps(E, reads, writes)
        semcnt[0] += 16
        E.eng.dma_start(out=out, in_=in_).then_inc(sem, 16)
        tok = ("d", sem, semcnt[0])
        for b in reads:
            b.r["dma%d" % id(sem)] = tok
        for b in writes:
            b.w = tok
            b.r = {}
        return tok

    def mark(self, name):
        self.marks.append((name, getattr(self.E["pe"], "nops", 0)))

    def wait_tok(self, ename, tok):
        self._wait(self.E[ename], tok, True)


def build(n_tiles=NTILES, dbg=False):
    nc = bass.Bass("TRN2", target_bir_lowering=False)
    S = Sched(nc)

    def dram_in(name, shape, dt=F32):
        return nc.dram_tensor(name, list(shape), dt, kind="ExternalInput").ap()

    x_d = dram_in("x", [SEQ, D])
    wp_d = dram_in("wpack", [NSLOT, 128, 2048])
    ident_d = dram_in("ident", [128, 128])
    gcols_d = dram_in("gcols", [128, 24])
    gfin_d = dram_in("gfin", [128, D])
    qkg_d = dram_in("qkg", [128, 2 * 16])
    lamv_d = dram_in("lamv", [128, 4 * 64])
    gsub_d = dram_in("gsub", [128, 1])
    bg_d = dram_in("bg", [128, 8, CW])
    mka_d = dram_in("mka", [128, CW])
    cs_d = dram_in("cs", [128, 32 * 32])
    qkgc_d = dram_in("qkgc", [128, 6])
    out_d = nc.dram_tensor("out", [SEQ, D], F32, kind="ExternalOutput").ap()

    def sb(name, shape, dt=F32):
        return nc.sbuf_tensor("sb_" + name, list(shape), dt).__enter__()

    def new_sem(name):
        return nc.semaphore(name).__enter__()

    identb = sb("identb", [128, 128], BF16); b_identb = Buf("identb")
    onesb = sb("onesb", [128, 128], BF16); b_onesb = Buf("onesb")
    onesf = sb("onesf", [128, 128]); b_onesf = Buf("onesf")
    gcols = sb("gcols", [128, 24]); b_gcols = Buf("gcols")
    gfin = sb("gfin", [128, D]); b_gfin = Buf("gfin")
    qkg = sb("qkg", [128, 2 * 16]); b_qkg = Buf("qkg")
    gsub8 = sb("gsub8", [128, 1]); b_gsub8 = Buf("gsub8")
    neglam = sb("neglam", [128, 1]); b_neglam = Buf("neglam")
    epsc = sb("epsc", [128, 1]); b_epsc = Buf("epsc")
    Ctab = sb("Ctab", [128, 8, CW], BF16); b_Ctab = Buf("Ctab")
    cs = sb("cs", [128, NS, 32]); b_cs = Buf("cs")
    KbT = sb("KbT", [128, 4, SEQ], BF16); b_KbT = [Buf("KbT%d" % t) for t in range(NTILES)]
    Vb = sb("Vb", [128, 32, 512], BF16); b_Vb = [Buf("Vb%d" % t) for t in range(NTILES)]
    KaT = sb("KaT", [128, 4, 2 * NT], BF16); b_KaT = [Buf("KaT0"), Buf("KaT1")]
    Va = sb("Va", [128, 8, 512], BF16); b_Va = [Buf("Va0"), Buf("Va1")]
    X = sb("X", [128, NS, D]); b_X = [[Buf("X%d_%d" % (s, n)) for n in range(2)] for s in range(NS)]
    hT = sb("hT", [128, NKC, NT], BF16); b_hT = [Buf("hT%d" % k) for k in range(NKC)]
    ACTB = sb("ACTB", [128, NFC, NT], BF16); b_ACTB = [Buf("ACTB%d" % k) for k in range(NFC)]
    qnb = [ACTB[:, 16 + i, :] for i in range(6)]; b_qnb = b_ACTB[16:22]
    wring = [sb("wring%d" % i, [128, 2048], BF16) for i in range(NW)]
    b_wring = [Buf("wring%d" % i) for i in range(NW)]
    w_sem = [new_sem("wsem%d" % i) for i in range(NW)]
    w_cnt = [[0] for _ in range(NW)]
    G = [sb("G%d" % i, [128, NT]) for i in range(5)]; b_G = [Buf("G%d" % i) for i in range(5)]
    sgt, b_sgt = G[0:2], b_G[0:2]
    qraw, b_qraw = G[0:2], b_G[0:2]
    sqt, b_sqt = G[2], b_G[2]
    rl, b_rl = G[3:5], b_G[3:5]
    o12, b_o12 = G[0:2], b_G[0:2]
    xs = [sb("xs%d" % i, [128, D]) for i in range(2)]; b_xs = [Buf("xs0"), Buf("xs1")]
    xnb = [sb("xnb%d" % i, [128, D], BF16) for i in range(2)]; b_xnb = [Buf("xnb0"), Buf("xnb1")]
    rt1 = [sb("rt1_%d" % i, [128, 8, 16]) for i in range(2)]; b_rt1 = [Buf("rt1_0"), Buf("rt1_1")]
    rt2 = [sb("rt2_%d" % i, [128, 8, 16]) for i in range(2)]; b_rt2 = [Buf("rt2_0"), Buf("rt2_1")]
    qkgc = sb("qkgc", [128, 6]); b_qkgc = Buf("qkgc")
    qaT = sb("qaT", [128, 4, NT], BF16); b_qaT = [Buf("qaT%d" % k) for k in range(4)]
    qbT = sb("qbT", [128, 8, NT], BF16); b_qbT = [Buf("qbT%d" % k) for k in range(4)]
    oT = sb("oT", [128, 8, NT], BF16); b_oT = [Buf("oT%d" % k) for k in range(8)]
    st8 = sb("st8", [128, 3, 8]); b_st8 = Buf("st8")
    st4 = sb("st4", [128, 3, NS]); b_st4 = [Buf("st4_%d" % s) for s in range(NS)]

    ps = [nc.psum_tensor("ps%d" % i, [128, 512], F32).__enter__() for i in range(8)]
    b_ps = [Buf("ps%d" % i) for i in range(8)]

    x_sem = new_sem("xld"); x_cnt = [0]
    xs_sem = [new_sem("xs0"), new_sem("xs1")]; xs_cnt = [[0], [0]]
    cs_sem = new_sem("cs"); cs_cnt = [0]
    st_sem = [new_sem("st0"), new_sem("st1")]; st_cnt = [[0], [0]]

    dbg_sem = new_sem("dbg"); dbg_cnt = [0]

    def dump(name, ap, bufs, shape, dt=F32):
        if not dbg:
            return
        d = nc.dram_tensor(name, list(shape), dt, kind="ExternalOutput").ap()
        tok = S.dma("sp", d, ap, dbg_sem, dbg_cnt, reads=bufs)
        S.wait_tok("sp", tok)

    wstate = {"issued": 0, "total": n_tiles * NSLOT}

    def w_issue_upto(k):
        while wstate["issued"] <= k and wstate["issued"] < wstate["total"]:
            u = wstate["issued"]
            r = u % NW
            S.dma("pool", wring[r][:], wp_d[u % NSLOT], w_sem[r], w_cnt[r], writes=[b_wring[r]])
            wstate["issued"] += 1

    wuse = {"k": 0}

    def w_next(held=0):
        k = wuse["k"]
        wuse["k"] += 1
        w_issue_upto(k + NW - 1 - held)
        r = k % NW
        return wring[r], b_wring[r]

    csems = {"n": 0}

    def cload(tile_ap, src, bufs, sem=None, cnt=None):
        if sem is None:
            sem = new_sem("c%d" % csems["n"]); cnt = [0]
            csems["n"] += 1
        S.dma("sp", tile_ap, src, sem, cnt, writes=bufs)

    cload(xs[0][:, 0:128], ident_d, [b_xs[0]], xs_sem[0], xs_cnt[0])
    cload(gcols[:], gcols_d, [b_gcols])
    S.op("dve", lambda: nc.vector.memset(epsc[:], EPS), writes=[b_epsc])
    S.op("dve", lambda: nc.vector.tensor_copy(out=identb[:], in_=xs[0][:, 0:128]), reads=[b_xs[0]], writes=[b_identb])
    S.op("dve", lambda: nc.vector.memset(onesb[:], 1.0), writes=[b_onesb])
    S.op("dve", lambda: nc.vector.memset(qbT[:], 0.0), writes=b_qbT)
    S.op("dve", lambda: nc.vector.memset(onesf[:], 1.0), writes=[b_onesf])

    def late_setup():
        cload(gfin[:], gfin_d, [b_gfin])
        cload(qkg[:], qkg_d, [b_qkg])
        cload(qkgc[:], qkgc_d, [b_qkgc])
        cload(gsub8[:], gsub_d, [b_gsub8])
        cload(X[:, 1, 0:256], lamv_d, b_X[1])
        cload(X[:, 0, 0:CW], mka_d, b_X[0])
        S.op("dve", lambda: nc.vector.tensor_scalar(out=qkgc[:, 0:1], in0=qkgc[:, 0:1], scalar1=0.125, scalar2=None,
                                                    op0=ALU.mult), reads=[b_qkgc], writes=[b_qkgc])
        S.op("dve", lambda: nc.vector.tensor_scalar(out=gsub8[:], in0=gsub8[:], scalar1=1.0 - LAMBDA_INIT,
                                                    scalar2=None, op0=ALU.mult), reads=[b_gsub8], writes=[b_gsub8])
        lv = X[:, 1, :]
        S.op("dve", lambda: nc.vector.tensor_tensor(out=lv[:, 256:320], in0=lv[:, 0:64], in1=lv[:, 64:128],
                                                    op=ALU.mult), reads=b_X[1], writes=b_X[1])
        S.op("dve", lambda: nc.vector.tensor_tensor(out=lv[:, 320:384], in0=lv[:, 128:192], in1=lv[:, 192:256],
                                                    op=ALU.mult), reads=b_X[1], writes=b_X[1])
        S.op("dve", lambda: nc.vector.tensor_reduce(out=st8[:, 0, 0:2],
                                                    in_=lv[:, 256:384].rearrange("p (a d) -> p a d", a=2),
                                                    axis=AX.X, op=ALU.add), reads=b_X[1], writes=[b_st8])
        S.op("act", lambda: nc.scalar.activation(out=st8[:, 1, 0:2], in_=st8[:, 0, 0:2], func=AF.Exp),
             reads=[b_st8], writes=[b_st8])
        S.op("dve", lambda: nc.vector.scalar_tensor_tensor(out=neglam[:], in0=st8[:, 1, 1:2], scalar=-LAMBDA_INIT,
                                                           in1=st8[:, 1, 0:1], op0=ALU.add, op1=ALU.subtract),
             reads=[b_st8], writes=[b_neglam])
        for h in range(8):
            k = h % 2
            cload(xs[k][:, 0:CW], bg_d[:, h, :], [b_xs[k]], xs_sem[k], xs_cnt[k])
            S.op("dve", lambda h=h, k=k: nc.vector.tensor_tensor(out=Ctab[:, h, :], in0=xs[k][:, 0:CW],
                                                                 in1=X[:, 0, 0:CW], op=ALU.add),
                 reads=[b_xs[k]] + b_X[0], writes=[b_Ctab])

    psrr = {"i": 0}

    def ps_next():
        i = psrr["i"]
        psrr["i"] = (i + 1) % 8
        return i

    identb_ap = identb[:]

    def norm_A(s, src, src_bufs, k):
        S.op("act", lambda: nc.scalar.activation(out=xnb[k][:], in_=src, func=AF.Square,
                                                 accum_out=st4[:, 0, s:s + 1]),
             reads=src_bufs, writes=[b_xnb[k], b_st4[s]])
        S.op("act", lambda: nc.scalar.activation(out=st4[:, 1, s:s + 1], in_=st4[:, 0, s:s + 1], func=AF.Ln,
                                                 scale=1.0 / D, bias=epsc[:]),
             reads=[b_st4[s], b_epsc], writes=[b_st4[s]])
        S.op("act", lambda: nc.scalar.activation(out=st4[:, 2, s:s + 1], in_=st4[:, 1, s:s + 1], func=AF.Exp,
                                                 scale=-0.5),
             reads=[b_st4[s]], writes=[b_st4[s]])
        S.op("dve", lambda: nc.vector.tensor_scalar(out=xnb[k][:], in0=src, scalar1=st4[:, 2, s:s + 1],
                                                    scalar2=None, op0=ALU.mult),
             reads=list(src_bufs) + [b_st4[s]], writes=[b_xnb[k]])

    def norm_B(s, k, gi):
        pi = ps_next()
        pb = ps[pi][:].bitcast(BF16)
        for kc in range(NKC):
            S.op("pe", lambda: nc.tensor.transpose(pb[:, kc * 128:(kc + 1) * 128], xnb[k][:, kc * 128:(kc + 1) * 128],
                                                   identb_ap),
                 reads=[b_xnb[k], b_identb], writes=[b_ps[pi]], inc=(kc == NKC - 1))
        gsl = gcols[:, gi * 8:gi * 8 + 8].unsqueeze(2).broadcast_to([128, 8, 128])
        S.op("dve", lambda: nc.vector.tensor_tensor(out=hT[:, :, s * 128:(s + 1) * 128],
                                                    in0=pb.rearrange("p (c t) -> p c t", c=8), in1=gsl, op=ALU.mult),
             reads=[b_ps[pi], b_gcols], writes=b_hT)

    def norm_to_hT(gi):
        for s in range(NS + 1):
            if s < NS:
                norm_A(s, X[:, s, :], b_X[s], s % 2)
            if s >= 1:
                norm_B(s - 1, (s - 1) % 2, gi)

    def prenorm_load(t, s):
        k = s % 2
        r0 = t * NT + s * 128
        S.dma("sp", xs[k][:], x_d[r0:r0 + 128, :], xs_sem[k], xs_cnt[k], writes=[b_xs[k]])

    def prenorm_A(t, s):
        norm_A(s, xs[s % 2][:], [b_xs[s % 2]], s % 2)

    def ffn_gu(hooks=None):
        for j in range(NFC):
            if hooks is not None and j in hooks:
                hooks[j]()
            wt, wb = w_next()
            wv = wt[:].rearrange("p (k c) -> p k c", k=8)
            pg = ps_next()
            pu = ps_next()
            for kc in range(NKC):
                S.op("pe", lambda: nc.tensor.matmul(ps[pg][:], lhsT=wv[:, kc, 0:128], rhs=hT[:, kc, :],
                                                    start=(kc == 0), stop=(kc == NKC - 1)),
                     reads=[wb, b_hT[kc]], writes=[b_ps[pg]], inc=(kc == NKC - 1))
            for kc in range(NKC):
                S.op("pe", lambda: nc.tensor.matmul(ps[pu][:], lhsT=wv[:, kc, 128:256], rhs=hT[:, kc, :],
                                                    start=(kc == 0), stop=(kc == NKC - 1)),
                     reads=[wb, b_hT[kc]], writes=[b_ps[pu]], inc=(kc == NKC - 1))
            k = j % 2
            S.op("act", lambda: nc.scalar.activation(out=sgt[k][:], in_=ps[pg][:], func=AF.Silu),
                 reads=[b_ps[pg]], writes=[b_sgt[k]])
            S.op("dve", lambda: nc.vector.tensor_tensor(out=ACTB[:, j, :], in0=ps[pu][:], in1=sgt[k][:], op=ALU.mult),
                 reads=[b_ps[pu], b_sgt[k]], writes=[b_ACTB[j]])

    def ffn_down(n, mid=None):
        pd = [ps_next() for _ in range(NS)]
        for g in range(6):
            if mid is not None and g == 3:
                mid()
            wt, wb = w_next()
            wv = wt[:].rearrange("p (k c) -> p k c", k=4)
            nk = 4 if g < 5 else 2
            for k in range(nk):
                kc = g * 4 + k
                for s in range(NS):
                    S.op("pe", lambda: nc.tensor.matmul(ps[pd[s]][:], lhsT=ACTB[:, kc, s * 128:(s + 1) * 128],
                                                        rhs=wv[:, k, :], start=(kc == 0), stop=(kc == NFC - 1)),
                         reads=[wb, b_ACTB[kc]], writes=[b_ps[pd[s]]],
                         inc=(kc == NFC - 1) or (k == nk - 1 and s == NS - 1))
        for s in range(NS):
            xsl = X[:, s, n * 512:(n + 1) * 512]
            S.op("dve", lambda: nc.vector.scalar_tensor_tensor(out=xsl, in0=ps[pd[s]][:], scalar=0.5, in1=xsl,
                                                               op0=ALU.mult, op1=ALU.add),
                 reads=[b_ps[pd[s]], b_X[s][n]], writes=[b_X[s][n]])

    st8x = [sb("st8x%d" % i, [128, 3, 8]) for i in range(2)]; b_st8x = [Buf("st8x0"), Buf("st8x1")]
    sqt2 = [G[2], G[3]]; b_sqt2 = [b_G[2], b_G[3]]
    rt0 = [sb("rt0_%d" % i, [128, 8, 16]) for i in range(2)]; b_rt0 = [Buf("rt0_0"), Buf("rt0_1")]

    def qk_stage1(pi, j, gain_idx, rotary, ksub):
        S.op("act", lambda: nc.scalar.activation(out=sqt2[j][:], in_=ps[pi][:], func=AF.Square),
             reads=[b_ps[pi]], writes=[b_sqt2[j]])
        S.op("dve", lambda: nc.vector.tensor_reduce(out=st8x[j][:, 0, :],
                                                    in_=sqt2[j][:].rearrange("p (h d) -> p h d", h=8),
                                                    axis=AX.X, op=ALU.add), reads=[b_sqt2[j]], writes=[b_st8x[j]])
        S.op("act", lambda: nc.scalar.activation(out=st8x[j][:, 1, :], in_=st8x[j][:, 0, :], func=AF.Ln,
                                                 scale=1.0 / 64, bias=epsc[:]),
             reads=[b_st8x[j], b_epsc], writes=[b_st8x[j]])
        S.op("act", lambda: nc.scalar.activation(out=st8x[j][:, 2, :], in_=st8x[j][:, 1, :], func=AF.Exp, scale=-0.5),
             reads=[b_st8x[j]], writes=[b_st8x[j]])
        if not rotary:
            return
        pv = ps[pi][:].rearrange("p (h d) -> p h d", h=8)
        r16, t1, t2 = rt0[j], rt1[j], rt2[j]
        gv = qkg[:, (gain_idx - 2) * 16:(gain_idx - 2) * 16 + 16].unsqueeze(1).broadcast_to([128, 8, 16])
        S.op("dve", lambda: nc.vector.tensor_tensor(out=r16[:], in0=pv[:, :, 0:16], in1=gv, op=ALU.mult),
             reads=[b_ps[pi], b_qkg], writes=[b_rt0[j]])
        cosv = cs[:, ksub, 0:16].unsqueeze(1).broadcast_to([128, 8, 16])
        sinA = cs[:, ksub, 16:24].unsqueeze(1).broadcast_to([128, 8, 8])
        sinB = cs[:, ksub, 24:32].unsqueeze(1).broadcast_to([128, 8, 8])
        S.op("dve", lambda: nc.vector.tensor_tensor(out=t1[:], in0=r16[:], in1=cosv, op=ALU.mult),
             reads=[b_rt0[j], b_cs], writes=[b_rt1[j]])
        S.op("dve", lambda: nc.vector.tensor_tensor(out=t2[:, :, 0:8], in0=r16[:, :, 8:16], in1=sinA, op=ALU.mult),
             reads=[b_rt0[j], b_cs], writes=[b_rt2[j]])
        S.op("dve", lambda: nc.vector.tensor_tensor(out=t2[:, :, 8:16], in0=r16[:, :, 0:8], in1=sinB, op=ALU.mult),
             reads=[b_rt0[j], b_cs], writes=[b_rt2[j]])
        S.op("dve", lambda: nc.vector.tensor_tensor(out=r16[:], in0=t1[:], in1=t2[:], op=ALU.add),
             reads=[b_rt1[j], b_rt2[j]], writes=[b_rt0[j]])

    def qk_stage2(pi, j, kq, rotary):
        rb = st8x[j][:, 2, :].unsqueeze(2).broadcast_to([128, 8, 64])
        pv = ps[pi][:].rearrange("p (h d) -> p h d", h=8)
        qo = qnb[kq][:].rearrange("p (h d) -> p h d", h=8)
        S.op("dve", lambda: nc.vector.tensor_tensor(out=qo, in0=pv, in1=rb, op=ALU.mult),
             reads=[b_ps[pi], b_st8x[j]], writes=[b_qnb[kq]])
        if rotary:
            S.op("dve", lambda: nc.vector.tensor_tensor(out=qo[:, :, 0:16], in0=rt0[j][:],
                                                        in1=st8x[j][:, 2, :].unsqueeze(2).broadcast_to([128, 8, 16]),
                                                        op=ALU.mult),
                 reads=[b_rt0[j], b_st8x[j]], writes=[b_qnb[kq]])

    def qk_tail(tb, kqs, gain_idx, rotary, dstT, dst_cols, dst_bufs):
        pb = ps[tb][:].bitcast(BF16)
        for i, kq in enumerate(kqs):
            for c in range(4):
                S.op("pe", lambda: nc.tensor.transpose(pb[:, i * 512 + c * 128:i * 512 + (c + 1) * 128],
                                                       qnb[kq][:, c * 128:(c + 1) * 128], identb_ap),
                     reads=[b_qnb[kq], b_identb], writes=[b_ps[tb]], inc=(i == 1 and c == 3))
        for i, kq in enumerate(kqs):
            src = pb[:, i * 512:(i + 1) * 512].rearrange("p (c t) -> p c t", c=4)
            gcol = gain_idx + 2 if rotary else gain_idx
            if dstT is qbT:
                dv = qbT[:].rearrange("p (h e) t -> p h e t", e=2)
                for e in range(2):
                    r0 = 64 * e
                    S.op("act", lambda: nc.scalar.activation(out=dv[r0:r0 + 64, :, e, dst_cols[i]:dst_cols[i] + 128],
                                                             in_=src[r0:r0 + 64], func=AF.Copy,
                                                             scale=qkgc[r0:r0 + 64, gcol:gcol + 1]),
                         reads=[b_ps[tb], b_qkgc], writes=dst_bufs)
                continue
            dst = dstT[:, :, dst_cols[i]:dst_cols[i] + 128]
            S.op("act", lambda: nc.scalar.activation(out=dst, in_=src, func=AF.Copy,
                                                     scale=qkgc[:, gcol:gcol + 1]),
                 reads=[b_ps[tb], b_qkgc], writes=dst_bufs)

    W_ORDER = (0, 1, 3, 2, 4, 5)
    MM_PAIRS = ((0, 1), (2, 3), (4, 5))

    def win_proj(t):
        q0 = t * NT
        ring = t % 2
        pending = []
        u = 0
        for g in W_ORDER:
            wA, bA = w_next()
            wB, bB = w_next(held=1)
            wvs = [wA[:].rearrange("p (k c) -> p k c", k=4), wB[:].rearrange("p (k c) -> p k c", k=4)]
            wbs = [bA, bB]
            for sp in range(2):
                banks = MM_PAIRS[u % 3]
                subs = (2 * sp, 2 * sp + 1)
                for kc in range(NKC):
                    half, k = divmod(kc, 4)
                    for i, s in enumerate(subs):
                        S.op("pe", lambda: nc.tensor.matmul(ps[banks[i]][:], lhsT=hT[:, kc, s * 128:(s + 1) * 128],
                                                            rhs=wvs[half][:, k, :], start=(kc == 0), stop=(kc == NKC - 1)),
                             reads=[wbs[half], b_hT[kc]], writes=[b_ps[banks[i]]],
                             inc=(kc == NKC - 1) or (sp == 1 and k == 3 and i == 1))
                while pending and pending[0][0] <= u - 2:
                    pending.pop(0)[1]()
                if g in (0, 1, 3, 4):
                    rotary = g in (3, 4)
                    gain_idx = {0: 0, 1: 1, 3: 2, 4: 3}[g]
                    kqs = ((2 * u) % 6, (2 * u + 1) % 6)
                    for i, s in enumerate(subs):
                        qk_stage1(banks[i], i, gain_idx, rotary, s)
                    for i, s in enumerate(subs):
                        qk_stage2(banks[i], i, kqs[i], rotary)
                    if g == 0:
                        dstT, cols, dbufs = qaT, [s * 128 for s in subs], b_qaT
                    elif g == 1:
                        dstT, cols, dbufs = KaT, [ring * NT + s * 128 for s in subs], [b_KaT[ring]]
                    elif g == 3:
                        dstT, cols, dbufs = qbT, [s * 128 for s in subs], b_qbT
                    else:
                        dstT, cols, dbufs = KbT, [q0 + s * 128 for s in subs], [b_KbT[t]]
                    tb = 6 + (u % 2)
                    pending.append((u, lambda tb=tb, kqs=kqs, gain_idx=gain_idx, rotary=rotary, dstT=dstT, cols=cols,
                                    dbufs=dbufs: qk_tail(tb, kqs, gain_idx, rotary, dstT, cols, dbufs)))
                else:
                    for i, s in enumerate(subs):
                        ktg = t * NS + s
                        if g == 2:
                            S.op("dve", lambda: nc.vector.tensor_copy(out=Va[:, ring * 4 + s, :], in_=ps[banks[i]][:]),
                                 reads=[b_ps[banks[i]]], writes=[b_Va[ring]])
                        else:
                            S.op("act", lambda: nc.scalar.activation(out=Vb[:, ktg, :], in_=ps[banks[i]][:], func=AF.Copy),
                                 reads=[b_ps[banks[i]]], writes=[b_Vb[t]])
                u += 1
        for i in range(8):
            if i in (1, 3) and pending:
                pending.pop(0)[1]()
            wt, wb = w_next()
            wv = wt[:].rearrange("p (k c) -> p k c", k=8)
            for cc in range(2):
                c = 2 * i + cc
                pg = ps_next()
                for kc in range(NKC):
                    S.op("pe", lambda: nc.tensor.matmul(ps[pg][:], lhsT=wv[:, kc, cc * 128:(cc + 1) * 128],
                                                        rhs=hT[:, kc, :], start=(kc == 0), stop=(kc == NKC - 1)),
                         reads=[wb, b_hT[kc]], writes=[b_ps[pg]], inc=(kc == NKC - 1))
                S.op("act", lambda: nc.scalar.activation(out=ACTB[:, c, :], in_=ps[pg][:], func=AF.Sigmoid),
                     reads=[b_ps[pg]], writes=[b_ACTB[c]])
        while pending:
            pending.pop(0)[1]()

    LAG = 2
    S_BANKS = [0, 1, 2, 3]
    E_CH = [16, 17, 18, 19, 20, 21]

    def attention(t):
        q0 = t * NT
        items = []
        nkt = q0 // 128 + 4
        for c in range(8):
            for kt in range(nkt):
                jd = kt - q0 // 128
                qa = 128 * jd if jd > 0 else 0
                items.append(dict(kind="B", grp=c, sub=0, h=c // 2, e=c % 2, kt=kt, diag=(jd >= 0), qa=qa, qb=512,
                                  first=(kt == 0), last=(kt == nkt - 1), gfirst=(kt == 0), glast=(kt == nkt - 1)))
        itemsA = []
        for hp in range(4):
            js = [j for j in (4, 3, 5, 2, 6, 1, 7, 0) if q0 - 512 + 128 * j >= 0]
            for idx, j in enumerate(js):
                k0 = q0 - 512 + 128 * j
                u0 = 512 - 128 * j
                qa = max(0, -u0)
                qb = min(512, CW - u0)
                itemsA.append(dict(kind="A2", grp=hp, k0=k0, u0=u0, qa=qa, qb=qb,
                                   first=(idx == 0), last=(idx == len(js) - 1),
                                   gfirst=(idx == 0), glast=(idx == len(js) - 1)))
        st = {"sb": 0, "eb": 0, "acc": 0, "step": 0}
        deferred = []
        DEFER = max(3, min(2 * (q0 // 128 + 4) - 4, 12))
        acc_banks = [(4, 5), (6, 7)]

        def emit_S(it):
            pi = S_BANKS[st["sb"] % len(S_BANKS)]
            st["sb"] += 1
            ei = E_CH[st["eb"] % len(E_CH)]
            st["eb"] += 1
            it["pi"] = pi
            it["ei"] = ei
            qa, qb = it["qa"], it["qb"]
            if it["kind"] == "A":
                hp, e, h = it["grp"], it["sub"], it["h"]
                r0 = 64 * e
                k0 = it["k0"]
                tk = k0 // NT
                col = (tk % 2) * NT + (k0 % NT)
                S.op("pe", lambda: nc.tensor.matmul(ps[pi][:, qa:qb], lhsT=KaT[r0:r0 + 64, hp, col:col + 128],
                                                    rhs=qaT[r0:r0 + 64, hp, qa:qb], start=True, stop=False),
                     reads=[b_KaT[tk % 2], b_qaT[hp]], writes=[b_ps[pi]], inc=False)
                u0 = it["u0"]
                S.op("pe", lambda: nc.tensor.matmul(ps[pi][:, qa:qb], lhsT=identb[:], rhs=Ctab[:, h, u0 + qa:u0 + qb],
                                                    start=False, stop=True),
                     reads=[b_identb, b_Ctab], writes=[b_ps[pi]], inc=True)
            else:
                h, e, kt = it["h"], it["e"], it["kt"]
                c = 2 * h + e
                S.op("pe", lambda: nc.tensor.matmul(ps[pi][:, qa:qb], lhsT=KbT[:, h, kt * 128:(kt + 1) * 128],
                                                    rhs=qbT[:, c, qa:qb], start=True, stop=True),
                     reads=[b_KbT[kt // NS], b_qbT[h]], writes=[b_ps[pi]], inc=True)
            S.op("act", lambda: nc.scalar.activation(out=ACTB[:, ei, 0:qb - qa], in_=ps[pi][:, qa:qb], func=AF.Exp,
                                                     scale=(1.0 if it["kind"] == "A" else 0.125)),
                 reads=[b_ps[pi]], writes=[b_ACTB[ei]])
            if it["kind"] == "B" and it["diag"]:
                S.op("act", lambda: nc.scalar.activation(out=ACTB[64:128, ei, 0:64], in_=ACTB[64:128, ei, 0:64],
                                                         func=AF.Copy, scale=0.0),
                     reads=[b_ACTB[ei]], writes=[b_ACTB[ei]])

        def emit_PV(it):
            ei = it["ei"]
            qa, qb = it["qa"], it["qb"]
            if it["gfirst"]:
                st["cur"] = acc_banks[st["acc"] % 2]
                st["acc"] += 1
            po, pl = st["cur"]
            if it["kind"] == "A":
                e, h = it["sub"], it["h"]
                r0 = 64 * e
                k0 = it["k0"]
                tk = k0 // NT
                vi = (tk % 2) * 4 + (k0 % NT) // 128
                S.op("pe", lambda: nc.tensor.matmul(ps[po][r0:r0 + 64, qa:qb], lhsT=Va[:, vi, h * 64:(h + 1) * 64],
                                                    rhs=ACTB[:, ei, 0:qb - qa], start=it["first"], stop=it["last"]),
                     reads=[b_Va[tk % 2], b_ACTB[ei]], writes=[b_ps[po]], inc=False)
                S.op("pe", lambda: nc.tensor.matmul(ps[pl][r0:r0 + 64, qa:qb], lhsT=onesb[:, 0:64],
                                                    rhs=ACTB[:, ei, 0:qb - qa], start=it["first"], stop=it["last"]),
                     reads=[b_onesb, b_ACTB[ei]], writes=[b_ps[pl]], inc=True)
            else:
                h, kt = it["h"], it["kt"]
                S.op("pe", lambda: nc.tensor.matmul(ps[po][:, qa:qb], lhsT=Vb[:, kt, h * 128:(h + 1) * 128],
                                                    rhs=ACTB[:, ei, 0:qb - qa], start=it["first"], stop=it["last"]),
                     reads=[b_Vb[kt // NS], b_ACTB[ei]], writes=[b_ps[po]], inc=False)
                S.op("pe", lambda: nc.tensor.matmul(ps[pl][:, qa:qb], lhsT=onesb[:], rhs=ACTB[:, ei, 0:qb - qa],
                                                    start=it["first"], stop=it["last"]),
                     reads=[b_onesb, b_ACTB[ei]], writes=[b_ps[pl]], inc=True)
            if not it["glast"]:
                return
            if it["kind"] == "A":
                hp = it["grp"]
                k = hp % 2
                S.op("dve", lambda: nc.vector.reciprocal(out=rl[k][:], in_=ps[pl][:]), reads=[b_ps[pl]], writes=[b_rl[k]])
                S.op("dve", lambda: nc.vector.tensor_tensor(out=oT[:, hp, :], in0=ps[po][:], in1=rl[k][:], op=ALU.mult),
                     reads=[b_ps[po], b_rl[k]], writes=[b_oT[hp]])
            else:
                c = it["grp"]
                h, e = c // 2, c % 2
                k = e
                if e == 1:
                    while deferred:
                        deferred.pop(0)[1]()
                S.op("dve", lambda: nc.vector.reciprocal(out=rl[k][:], in_=ps[pl][:]), reads=[b_ps[pl]], writes=[b_rl[k]])
                S.op("dve", lambda: nc.vector.tensor_tensor(out=o12[e][:], in0=ps[po][:], in1=rl[k][:], op=ALU.mult),
                     reads=[b_ps[po], b_rl[k]], writes=[b_o12[e]])
                if e == 1:
                    S.op("dve", lambda: nc.vector.scalar_tensor_tensor(out=sqt[:], in0=o12[1][:], scalar=neglam[:],
                                                                       in1=o12[0][:], op0=ALU.mult, op1=ALU.add),
                         reads=[b_o12[0], b_o12[1], b_neglam], writes=[b_sqt])

                    hold = {}

                    def tail1(h=h):
                        S.op("dve", lambda: nc.vector.tensor_tensor(out=o12[1][:], in0=sqt[:], in1=sqt[:], op=ALU.mult),
                             reads=[b_sqt], writes=[b_o12[1]])
                        pr = S_BANKS[st["sb"] % len(S_BANKS)]
                        st["sb"] += 1
                        hold["pr"] = pr
                        S.op("pe", lambda: nc.tensor.matmul(ps[pr][:], lhsT=onesf[:], rhs=o12[1][:], start=True, stop=True),
                             reads=[b_onesf, b_o12[1]], writes=[b_ps[pr]], inc=True)

                    def tail2(h=h):
                        pr = hold["pr"]
                        S.op("act", lambda: nc.scalar.activation(out=rl[0][:], in_=ps[pr][:], func=AF.Ln,
                                                                 scale=1.0 / 128, bias=epsc[:]),
                             reads=[b_ps[pr], b_epsc], writes=[b_rl[0]])

                    def tail3(h=h):
                        S.op("act", lambda: nc.scalar.activation(out=rl[1][:], in_=rl[0][:], func=AF.Exp, scale=-0.5),
                             reads=[b_rl[0]], writes=[b_rl[1]])
                        S.op("dve", lambda: nc.vector.scalar_tensor_tensor(out=oT[:, 4 + h, :], in0=sqt[:],
                                                                           scalar=gsub8[:], in1=rl[1][:],
                                                                           op0=ALU.mult, op1=ALU.mult),
                             reads=[b_sqt, b_gsub8, b_rl[1]], writes=[b_oT[4 + h]])
                    deferred.append([st["step"] + DEFER, tail1])
                    deferred.append([st["step"] + DEFER + 1, lambda: (tail2(), tail3())])

        def emit_S2(it):
            hp = it["grp"]
            qa, qb = it["qa"], it["qb"]
            k0 = it["k0"]
            u0 = it["u0"]
            tk = k0 // NT
            col = (tk % 2) * NT + (k0 % NT)
            pis, eis = [], []
            for e in range(2):
                pis.append(S_BANKS[st["sb"] % len(S_BANKS)]); st["sb"] += 1
                eis.append(E_CH[st["eb"] % len(E_CH)]); st["eb"] += 1
            it["pis"], it["eis"] = pis, eis
            for e in range(2):
                r0 = 64 * e
                S.op("pe", lambda: nc.tensor.matmul(ps[pis[e]][:, qa:qb], lhsT=KaT[r0:r0 + 64, hp, col:col + 128],
                                                    rhs=qaT[r0:r0 + 64, hp, qa:qb], start=True, stop=False),
                     reads=[b_KaT[tk % 2], b_qaT[hp]], writes=[b_ps[pis[e]]], inc=False)
            for e in range(2):
                h = 2 * hp + e
                S.op("pe", lambda: nc.tensor.matmul(ps[pis[e]][:, qa:qb], lhsT=identb[:],
                                                    rhs=Ctab[:, h, u0 + qa:u0 + qb], start=False, stop=True),
                     reads=[b_identb, b_Ctab], writes=[b_ps[pis[e]]], inc=True)
            for e in range(2):
                S.op("act", lambda: nc.scalar.activation(out=ACTB[:, eis[e], 0:qb - qa], in_=ps[pis[e]][:, qa:qb],
                                                         func=AF.Exp),
                     reads=[b_ps[pis[e]]], writes=[b_ACTB[eis[e]]])

        def emit_PV2(it):
            hp = it["grp"]
            qa, qb = it["qa"], it["qb"]
            eis = it["eis"]
            if it["gfirst"]:
                st["cur"] = acc_banks[st["acc"] % 2]
                st["acc"] += 1
            po, pl = st["cur"]
            k0 = it["k0"]
            tk = k0 // NT
            vi = (tk % 2) * 4 + (k0 % NT) // 128
            for e in range(2):
                h = 2 * hp + e
                r0 = 64 * e
                S.op("pe", lambda: nc.tensor.matmul(ps[po][r0:r0 + 64, qa:qb], lhsT=Va[:, vi, h * 64:(h + 1) * 64],
                                                    rhs=ACTB[:, eis[e], 0:qb - qa], start=it["first"], stop=it["last"]),
                     reads=[b_Va[tk % 2], b_ACTB[eis[e]]], writes=[b_ps[po]], inc=False)
            for e in range(2):
                r0 = 64 * e
                S.op("pe", lambda: nc.tensor.matmul(ps[pl][r0:r0 + 64, qa:qb], lhsT=onesb[:, 0:64],
                                                    rhs=ACTB[:, eis[e], 0:qb - qa], start=it["first"], stop=it["last"]),
                     reads=[b_onesb, b_ACTB[eis[e]]], writes=[b_ps[pl]], inc=(e == 1))
            if it["glast"]:
                k = hp % 2
                S.op("dve", lambda: nc.vector.reciprocal(out=rl[k][:], in_=ps[pl][:]), reads=[b_ps[pl]], writes=[b_rl[k]])
                S.op("dve", lambda: nc.vector.tensor_tensor(out=oT[:, hp, :], in0=ps[po][:], in1=rl[k][:], op=ALU.mult),
                     reads=[b_ps[po], b_rl[k]], writes=[b_oT[hp]])

        def run_deferred(i):
            while deferred and deferred[0][0] <= i:
                deferred.pop(0)[1]()

        n = len(items)
        nA = len(itemsA)
        ia = 0
        pa = 0
        step = 0
        for i in range(n + LAG):
            st["step"] = step
            if i < n:
                emit_S(items[i])
            elif i == n + LAG - 1 and ia < nA:
                emit_S2(itemsA[ia]); ia += 1
            if i - LAG >= 0:
                emit_PV(items[i - LAG])
            run_deferred(step)
            step += 1
        while pa < nA:
            st["step"] = step
            if ia < nA:
                emit_S2(itemsA[ia]); ia += 1
            emit_PV2(itemsA[pa]); pa += 1
            run_deferred(step)
            step += 1
        while deferred:
            deferred.pop(0)[1]()

    def merge():
        for half in range(2):
            wta, wba = w_next()
            wtb, wbb = w_next(held=1)
            wva = wta[:].rearrange("p (k c) -> p k c", k=4)
            wvb = wtb[:].rearrange("p (k c) -> p k c", k=4)
            for cc in range(4):
                c = half * 4 + cc
                pa = ps_next()
                pb = ps_next()
                for kc in range(4):
                    S.op("pe", lambda: nc.tensor.matmul(ps[pb][:], lhsT=wvb[:, kc, cc * 128:(cc + 1) * 128],
                                                        rhs=oT[:, 4 + kc, :], start=(kc == 0), stop=(kc == 3)),
                         reads=[wbb, b_oT[4 + kc]], writes=[b_ps[pb]], inc=(kc == 3))
                for kc in range(4):
                    S.op("pe", lambda: nc.tensor.matmul(ps[pa][:], lhsT=wva[:, kc, cc * 128:(cc + 1) * 128],
                                                        rhs=oT[:, kc, :], start=(kc == 0), stop=(kc == 3)),
                         reads=[wba, b_oT[kc]], writes=[b_ps[pa]], inc=(kc == 3))
                S.op("dve", lambda: nc.vector.tensor_tensor(out=o12[0][:], in0=ps[pa][:], in1=ACTB[:, c, :], op=ALU.mult),
                     reads=[b_ps[pa], b_ACTB[c]], writes=[b_o12[0]])
                S.op("dve", lambda: nc.vector.tensor_tensor(out=o12[1][:], in0=ps[pb][:], in1=ACTB[:, 8 + c, :],
                                                            op=ALU.mult),
                     reads=[b_ps[pb], b_ACTB[8 + c]], writes=[b_o12[1]])
                S.op("dve", lambda: nc.vector.tensor_tensor(out=hT[:, c, :], in0=o12[0][:], in1=o12[1][:], op=ALU.add),
                     reads=[b_o12[0], b_o12[1]], writes=[b_hT[c]])
        for n in range(2):
            pw = [ps_next() for _ in range(NS)]
            for half in range(2):
                wt, wb = w_next()
                wv = wt[:].rearrange("p (k c) -> p k c", k=4)
                for k in range(4):
                    kc = half * 4 + k
                    for s in range(NS):
                        S.op("pe", lambda: nc.tensor.matmul(ps[pw[s]][:], lhsT=hT[:, kc, s * 128:(s + 1) * 128],
                                                            rhs=wv[:, k, :], start=(kc == 0), stop=(kc == NKC - 1)),
                             reads=[wb, b_hT[kc]], writes=[b_ps[pw[s]]],
                             inc=(kc == NKC - 1) or (k == 3 and s == NS - 1))
            for s in range(NS):
                xs = X[:, s, n * 512:(n + 1) * 512]
                S.op("dve", lambda: nc.vector.tensor_tensor(out=xs, in0=ps[pw[s]][:], in1=xs, op=ALU.add),
                     reads=[b_ps[pw[s]], b_X[s][n]], writes=[b_X[s][n]])

    def final_sq(s):
        bx = b_X[s]
        k = s % 2
        S.op("act", lambda: nc.scalar.activation(out=xs[k][:], in_=X[:, s, :], func=AF.Square,
                                                 accum_out=st4[:, 0, s:s + 1]),
             reads=bx, writes=[b_xs[k], b_st4[s]])

    def final_rstd():
        S.op("act", lambda: nc.scalar.activation(out=st4[:, 1, :], in_=st4[:, 0, :], func=AF.Ln,
                                                 scale=1.0 / D, bias=epsc[:]),
             reads=b_st4 + [b_epsc], writes=b_st4)
        S.op("act", lambda: nc.scalar.activation(out=st4[:, 2, :], in_=st4[:, 1, :], func=AF.Exp, scale=-0.5),
             reads=b_st4, writes=b_st4)

    def final_store(t, s):
        q0 = t * NT
        k = s % 2
        S.op("dve", lambda: nc.vector.scalar_tensor_tensor(out=xs[k][:], in0=X[:, s, :], scalar=st4[:, 2, s:s + 1],
                                                           in1=gfin[:], op0=ALU.mult, op1=ALU.mult),
             reads=b_X[s] + [b_st4[s], b_gfin], writes=[b_xs[k]])
        return S.dma("sp", out_d[q0 + s * 128:q0 + (s + 1) * 128, :], xs[k][:], xs_sem[k], xs_cnt[k],
                     reads=[b_xs[k]])

    def final_hooks(t):
        def x_load():
            q1 = (t + 1) * NT
            S.dma("sp", X[:], x_d[q1:q1 + NT, :].rearrange("(s p) c -> p s c", p=128), x_sem, x_cnt, writes=allX)
        return {2: lambda: final_sq(0), 3: lambda: final_sq(1), 4: lambda: (final_rstd_pair(0), final_store(t, 0), final_store(t, 1)),
                5: lambda: final_sq(2), 6: lambda: final_sq(3), 7: lambda: (final_rstd_pair(2), final_store(t, 2), final_store(t, 3)),
                8: x_load}

    def final_rstd_pair(s0):
        S.op("act", lambda: nc.scalar.activation(out=st4[:, 1, s0:s0 + 2], in_=st4[:, 0, s0:s0 + 2], func=AF.Ln,
                                                 scale=1.0 / D, bias=epsc[:]),
             reads=[b_st4[s0], b_st4[s0 + 1], b_epsc], writes=[b_st4[s0], b_st4[s0 + 1]])
        S.op("act", lambda: nc.scalar.activation(out=st4[:, 2, s0:s0 + 2], in_=st4[:, 1, s0:s0 + 2], func=AF.Exp,
                                                 scale=-0.5),
             reads=[b_st4[s0], b_st4[s0 + 1]], writes=[b_st4[s0], b_st4[s0 + 1]])

    last_toks = []
    allX = [b for bs in b_X for b in bs]
    prenorm_load(0, 0); prenorm_load(0, 1)
    prenorm_A(0, 0); prenorm_A(0, 1)
    norm_B(0, 0, 0)
    prenorm_load(0, 2); prenorm_A(0, 2)
    norm_B(1, 1, 0)
    prenorm_load(0, 3); prenorm_A(0, 3)
    norm_B(2, 0, 0)
    norm_B(3, 1, 0)
    late_setup()
    for t in range(n_tiles):
        q0 = t * NT
        if t == 0:
            S.dma("sp", X[:], x_d[q0:q0 + NT, :].rearrange("(s p) c -> p s c", p=128), x_sem, x_cnt, writes=allX)
        if t == 0:
            dump("d_h0T", hT[:].rearrange("p a b -> p (a b)"), b_hT, [128, NKC * NT], BF16)
        S.mark('F1gu')
        ffn_gu(hooks=(final_hooks(t - 1) if t > 0 else None))
        S.mark('F1dn')
        ffn_down(0)
        ffn_down(1)
        S.mark('N2')
        if t == 0:
            dump("d_x1", X[:].rearrange("p a b -> p (a b)"), allX, [128, NS * D])
        norm_to_hT(1)
        S.mark('W')
        S.dma("sp", cs[:].rearrange("p a b -> p (a b)"), cs_d[:, t * 128:(t + 1) * 128], cs_sem, cs_cnt, writes=[b_cs])
        win_proj(t)
        S.mark('A')
        if t == 0:
            dump("d_qaT", qaT[:].rearrange("p a b -> p (a b)"), b_qaT, [128, 4 * NT], BF16)
            dump("d_KaT", KaT[:].rearrange("p a b -> p (a b)"), b_KaT, [128, 8 * NT], BF16)
            dump("d_KbT", KbT[:, :, 0:NT], [b_KbT[0]], [128, 4, NT], BF16)
            dump("d_Va", Va[:, 0:4, :], b_Va, [128, 4, 512], BF16)
            dump("d_Vb", Vb[:, 0:4, :], [b_Vb[0]], [128, 4, 512], BF16)
            dump("d_gate", ACTB[:, 0:16, :], b_ACTB[0:16], [128, 16, NT], BF16)
        attention(t)
        if t == 0:
            dump("d_oT", oT[:].rearrange("p a b -> p (a b)"), b_oT, [128, 8 * NT], BF16)
        S.mark('M')
        merge()
        S.mark('N3')
        if t == 0:
            dump("d_yT", hT[:].rearrange("p a b -> p (a b)"), b_hT, [128, NKC * NT], BF16)
            dump("d_x2", X[:].rearrange("p a b -> p (a b)"), allX, [128, NS * D])
        norm_to_hT(2)
        S.mark('F2gu')
        nxt = t + 1 < n_tiles
        ffn_gu()
        S.mark('F2dn')
        if nxt:
            prenorm_load(t + 1, 0); prenorm_load(t + 1, 1)
            prenorm_A(t + 1, 0); prenorm_A(t + 1, 1)
        ffn_down(0)
        if nxt:
            norm_B(0, 0, 0); norm_B(1, 1, 0)
            prenorm_load(t + 1, 2); prenorm_load(t + 1, 3)
            prenorm_A(t + 1, 2); prenorm_A(t + 1, 3)
        ffn_down(1, mid=((lambda: (norm_B(2, 0, 0), norm_B(3, 1, 0))) if nxt else None))
        if t == 0:
            dump("d_x3", X[:].rearrange("p a b -> p (a b)"), allX, [128, NS * D])
        S.mark('N4')
        if not nxt:
            for s_ in range(NS):
                final_sq(s_)
            final_rstd()
            last_toks = [final_store(t, s_) for s_ in range(NS)]
    for tok in last_toks:
        S.wait_tok("sp", tok)
    assert wuse["k"] == n_tiles * NSLOT, wuse["k"]
    nc._phase_marks = S.marks
    return nc


def _pack_weights(inp):
    slots = []

    def panel(W, cols_list):
        parts = [W[:, c0:c1].reshape(8, 128, c1 - c0).transpose(1, 0, 2) for (c0, c1) in cols_list]
        return np.concatenate(parts, axis=2).reshape(128, 2048)

    def rows(W, kc0, nk, c0):
        a = np.zeros((128, 4, 512), np.float32)
        a[:, :nk, :] = W[kc0 * 128:(kc0 + nk) * 128, c0:c0 + 512].reshape(nk, 128, 512).transpose(1, 0, 2)
        return a.reshape(128, 2048)

    def ffn_slots(wgu, wdn):
        for j in range(NFC):
            slots.append(panel(wgu, [(j * 128, (j + 1) * 128), (FF + j * 128, FF + (j + 1) * 128)]))
        for n in range(2):
            for g in range(6):
                slots.append(rows(wdn, g * 4, 4 if g < 5 else 2, n * 512))

    ffn_slots(inp["w_ffn1_gu"][0], inp["w_ffn1_down"][0])
    win = inp["w_in"][0]
    for g in (0, 1, 3, 2, 4, 5):
        for half in range(2):
            slots.append(rows(win, half * 4, 4, g * 512))
    for i in range(8):
        slots.append(panel(win, [(3072 + i * 256, 3072 + (i + 1) * 256)]))
    wa, wb, wo = inp["w_up_a"][0], inp["w_up_b"][0], inp["w_out"][0]
    for half in range(2):
        slots.append(rows(wa, 0, 4, half * 512))
        slots.append(rows(wb, 0, 4, half * 512))
    for n in range(2):
        for half in range(2):
            slots.append(rows(wo, half * 4, 4, n * 512))
    ffn_slots(inp["w_ffn2_gu"][0], inp["w_ffn2_down"][0])
    assert len(slots) == NSLOT
    return np.ascontiguousarray(np.stack(slots, 0))


def _consts(inp):
    f = np.float32
    c = {}
    c["ident"] = np.eye(128, dtype=f)
    g = np.stack([inp["g_ffn1"][0], inp["g_mix"][0], inp["g_ffn2"][0]], 0)
    c["gcols"] = np.ascontiguousarray(g.reshape(3, 8, 128).transpose(2, 0, 1).reshape(128, 24)).astype(f)
    c["gfin"] = np.ascontiguousarray(np.broadcast_to(inp["g_final"][0][None, :], (128, D))).astype(f)
    qk = np.concatenate([inp["qn_a"][0], inp["kn_a"][0], inp["qn_b"][0], inp["kn_b"][0]])
    qk16 = np.concatenate([inp["qn_b"][0][:16], inp["kn_b"][0][:16]])
    c["qkg"] = np.ascontiguousarray(np.broadcast_to(qk16[None, :], (128, 32))).astype(f)
    qk4 = qk.reshape(4, 64)
    dsel = (np.arange(64) < 16)[None, :]
    qk6 = np.concatenate([qk4, np.where(dsel, np.float32(1.0), qk4[2:4])], 0)
    c["qkgc"] = np.ascontiguousarray(np.tile(qk6.T, (2, 1))).astype(f)
    lv = np.concatenate([inp["lambda_q1"][0], inp["lambda_k1"][0], inp["lambda_q2"][0], inp["lambda_k2"][0]])
    c["lamv"] = np.ascontiguousarray(np.broadcast_to(lv[None, :], (128, 256))).astype(f)
    c["gsub"] = np.ascontiguousarray(inp["g_subln"][0].reshape(128, 1)).astype(f)
    ki = np.arange(128)[:, None]
    u = np.arange(CW)[None, :]
    idx = np.clip(u - ki, -256, 256) + 256
    rel = inp["rel_bias"][0]
    c["bg"] = np.ascontiguousarray(rel[idx].transpose(0, 2, 1)).astype(f)
    dch = (u // 64) - (ki // 64)
    c["mka"] = np.where((dch >= 0) & (dch <= 8), 0.0, NEG).astype(f)
    pos = np.arange(SEQ, dtype=f)
    inv = (np.float32(500000.0) ** (-np.arange(0, 16, 2, dtype=f) / np.float32(16))).astype(f)
    ang = (pos[:, None] * inv[None, :]).astype(f)
    cosv, sinv = np.cos(ang).astype(f), np.sin(ang).astype(f)
    tab = np.concatenate([cosv, cosv, -sinv, sinv], axis=1)
    c["cs"] = np.ascontiguousarray(tab.reshape(32, 128, 32).transpose(1, 0, 2).reshape(128, 32 * 32)).astype(f)
    return c


_NC_CACHE = {}


def kernel(**inputs):
    inp = {k: np.asarray(v, dtype=np.float32) for k, v in inputs.items()}
    x = inp["x"]
    B = x.shape[0]
    if "nc" not in _NC_CACHE:
        _NC_CACHE["nc"] = build()
    nc = _NC_CACHE["nc"]
    shared = _consts(inp)
    shared["wpack"] = _pack_weights(inp)
    in_maps = []
    for b in range(B):
        m = dict(shared)
        m["x"] = np.ascontiguousarray(x[b])
        in_maps.append(m)
    res = run_bass_kernel_spmd(nc, in_maps, core_ids=list(range(B)))
    out = np.stack([np.asarray(res.results[b]["out"], dtype=np.float32).reshape(SEQ, D) for b in range(B)], 0)
    return out
```

```python
import math
import numpy as np
import concourse.bass as bass
import concourse.mybir as mybir
from concourse.bass_utils import run_bass_kernel_spmd

F32 = mybir.dt.float32
BF16 = mybir.dt.bfloat16
AF = mybir.ActivationFunctionType
ALU = mybir.AluOpType
AX = mybir.AxisListType

SEQ = 4096
D = 1024
FF = 2816
NKC = 8
NFC = 22
NT = 512
NS = 4
NTILES = SEQ // NT
NW = 5
NSLOT = 96
EPS = 1e-6
NEG = -30000.0
LAMBDA_INIT = 0.8 - 0.6 * math.exp(-0.3 * 0)
CW = 640


class Buf:
    __slots__ = ("name", "w", "r")

    def __init__(self, name):
        self.name = name
        self.w = None
        self.r = {}


class Eng:
    def __init__(self, name, eng, sem, self_raw):
        self.name = name
        self.eng = eng
        self.sem = sem
        self.count = 0
        self.waited = {}
        self.self_raw = self_raw


class Sched:
    def __init__(self, nc):
        self.nc = nc
        self.marks = []
        self.E = {}
        for name, eng, sr in (("pe", nc.tensor, False), ("act", nc.scalar, True), ("dve", nc.vector, True),
                              ("pool", nc.gpsimd, True), ("sp", nc.sync, False)):
            sem = nc.semaphore("sem_" + name).__enter__()
            self.E[name] = Eng(name, eng, sem, sr)

    def _wait(self, E, tok, raw):
        if tok[0] == "e":
            src = self.E[tok[1]]
            val = tok[2]
            if src is E and not E.self_raw:
                return
            key = ("e", src.name)
            if E.waited.get(key, 0) >= val:
                return
            E.eng.wait_ge(src.sem, val)
            E.waited[key] = val
        else:
            sem = tok[1]
            val = tok[2]
            key = ("d", id(sem))
            if E.waited.get(key, 0) >= val:
                return
            E.eng.wait_ge(sem, val)
            E.waited[key] = val

    def _deps(self, E, reads, writes):
        for b in reads:
            if b.w is not None:
                self._wait(E, b.w, True)
        for b in writes:
            if b.w is not None:
                self._wait(E, b.w, False)
            for tok in b.r.values():
                self._wait(E, tok, False)

    def op(self, ename, fn, reads=(), writes=(), inc=True):
        E = self.E[ename]
        self._deps(E, reads, writes)
        inst = fn()
        E.nops = getattr(E, "nops", 0) + 1
        if inc:
            E.count += 1
            inst.then_inc(E.sem, 1)
            tok = ("e", ename, E.count)
        else:
            tok = ("e", ename, E.count + 1)
        for b in reads:
            b.r[ename] = tok
        for b in writes:
            b.w = tok
            b.r = {}
        return inst

    def dma(self, qname, out, in_, sem, semcnt, reads=(), writes=()):
        E = self.E[qname]
        self._deps(E, reads, writes)
        semcnt[0] += 16
        E.eng.dma_start(out=out, in_=in_).then_inc(sem, 16)
        tok = ("d", sem, semcnt[0])
        for b in reads:
            b.r["dma%d" % id(sem)] = tok
        for b in writes:
            b.w = tok
            b.r = {}
        return tok

    def mark(self, name):
        self.marks.append((name, getattr(self.E["pe"], "nops", 0)))

    def wait_tok(self, ename, tok):
        self._wait(self.E[ename], tok, True)


def build(n_tiles=NTILES, dbg=False):
    nc = bass.Bass("TRN2", target_bir_lowering=False)
    S = Sched(nc)

    def dram_in(name, shape, dt=F32):
        return nc.dram_tensor(name, list(shape), dt, kind="ExternalInput").ap()

    x_d = dram_in("x", [SEQ, D])
    wp_d = dram_in("wpack", [NSLOT, 128, 2048])
    ident_d = dram_in("ident", [128, 128])
    gcols_d = dram_in("gcols", [128, 24])
    gfin_d = dram_in("gfin", [128, D])
    qkg_d = dram_in("qkg", [128, 2 * 16])
    lamv_d = dram_in("lamv", [128, 4 * 64])
    gsub_d = dram_in("gsub", [128, 1])
    bg_d = dram_in("bg", [128, 8, CW])
    mka_d = dram_in("mka", [128, CW])
    cs_d = dram_in("cs", [128, 32 * 32])
    qkgc_d = dram_in("qkgc", [128, 6])
    out_d = nc.dram_tensor("out", [SEQ, D], F32, kind="ExternalOutput").ap()

    def sb(name, shape, dt=F32):
        return nc.sbuf_tensor("sb_" + name, list(shape), dt).__enter__()

    def new_sem(name):
        return nc.semaphore(name).__enter__()

    identb = sb("identb", [128, 128], BF16); b_identb = Buf("identb")
    onesb = sb("onesb", [128, 128], BF16); b_onesb = Buf("onesb")
    onesf = sb("onesf", [128, 128]); b_onesf = Buf("onesf")
    gcols = sb("gcols", [128, 24]); b_gcols = Buf("gcols")
    gfin = sb("gfin", [128, D]); b_gfin = Buf("gfin")
    qkg = sb("qkg", [128, 2 * 16]); b_qkg = Buf("qkg")
    gsub8 = sb("gsub8", [128, 1]); b_gsub8 = Buf("gsub8")
    neglam = sb("neglam", [128, 1]); b_neglam = Buf("neglam")
    epsc = sb("epsc", [128, 1]); b_epsc = Buf("epsc")
    Ctab = sb("Ctab", [128, 8, CW], BF16); b_Ctab = Buf("Ctab")
    cs = sb("cs", [128, NS, 32]); b_cs = Buf("cs")
    KbT = sb("KbT", [128, 4, SEQ], BF16); b_KbT = [Buf("KbT%d" % t) for t in range(NTILES)]
    Vb = sb("Vb", [128, 32, 512], BF16); b_Vb = [Buf("Vb%d" % t) for t in range(NTILES)]
    KaT = sb("KaT", [128, 4, 2 * NT], BF16); b_KaT = [Buf("KaT0"), Buf("KaT1")]
    Va = sb("Va", [128, 8, 512], BF16); b_Va = [Buf("Va0"), Buf("Va1")]
    X = sb("X", [128, NS, D]); b_X = [[Buf("X%d_%d" % (s, n)) for n in range(2)] for s in range(NS)]
    hT = sb("hT", [128, NKC, NT], BF16); b_hT = [Buf("hT%d" % k) for k in range(NKC)]
    ACTB = sb("ACTB", [128, NFC, NT], BF16); b_ACTB = [Buf("ACTB%d" % k) for k in range(NFC)]
    qnb = [ACTB[:, 16 + i, :] for i in range(6)]; b_qnb = b_ACTB[16:22]
    wring = [sb("wring%d" % i, [128, 2048], BF16) for i in range(NW)]
    b_wring = [Buf("wring%d" % i) for i in range(NW)]
    w_sem = [new_sem("wsem%d" % i) for i in range(NW)]
    w_cnt = [[0] for _ in range(NW)]
    G = [sb("G%d" % i, [128, NT]) for i in range(5)]; b_G = [Buf("G%d" % i) for i in range(5)]
    sgt, b_sgt = G[0:2], b_G[0:2]
    qraw, b_qraw = G[0:2], b_G[0:2]
    sqt, b_sqt = G[2], b_G[2]
    rl, b_rl = G[3:5], b_G[3:5]
    o12, b_o12 = G[0:2], b_G[0:2]
    xs = [sb("xs%d" % i, [128, D]) for i in range(2)]; b_xs = [Buf("xs0"), Buf("xs1")]
    xnb = [sb("xnb%d" % i, [128, D], BF16) for i in range(2)]; b_xnb = [Buf("xnb0"), Buf("xnb1")]
    rt1 = [sb("rt1_%d" % i, [128, 8, 16]) for i in range(2)]; b_rt1 = [Buf("rt1_0"), Buf("rt1_1")]
    rt2 = [sb("rt2_%d" % i, [128, 8, 16]) for i in range(2)]; b_rt2 = [Buf("rt2_0"), Buf("rt2_1")]
    qkgc = sb("qkgc", [128, 6]); b_qkgc = Buf("qkgc")
    qaT = sb("qaT", [128, 4, NT], BF16); b_qaT = [Buf("qaT%d" % k) for k in range(4)]
    qbT = sb("qbT", [128, 8, NT], BF16); b_qbT = [Buf("qbT%d" % k) for k in range(4)]
    oT = sb("oT", [128, 8, NT], BF16); b_oT = [Buf("oT%d" % k) for k in range(8)]
    st8 = sb("st8", [128, 3, 8]); b_st8 = Buf("st8")
    st4 = sb("st4", [128, 3, NS]); b_st4 = [Buf("st4_%d" % s) for s in range(NS)]

    ps = [nc.psum_tensor("ps%d" % i, [128, 512], F32).__enter__() for i in range(8)]
    b_ps = [Buf("ps%d" % i) for i in range(8)]

    x_sem = new_sem("xld"); x_cnt = [0]
    xs_sem = [new_sem("xs0"), new_sem("xs1")]; xs_cnt = [[0], [0]]
    cs_sem = new_sem("cs"); cs_cnt = [0]
    st_sem = [new_sem("st0"), new_sem("st1")]; st_cnt = [[0], [0]]

    dbg_sem = new_sem("dbg"); dbg_cnt = [0]

    def dump(name, ap, bufs, shape, dt=F32):
        if not dbg:
            return
        d = nc.dram_tensor(name, list(shape), dt, kind="ExternalOutput").ap()
        tok = S.dma("sp", d, ap, dbg_sem, dbg_cnt, reads=bufs)
        S.wait_tok("sp", tok)

    wstate = {"issued": 0, "total": n_tiles * NSLOT}

    def w_issue_upto(k):
        while wstate["issued"] <= k and wstate["issued"] < wstate["total"]:
            u = wstate["issued"]
            r = u % NW
            S.dma("pool", wring[r][:], wp_d[u % NSLOT], w_sem[r], w_cnt[r], writes=[b_wring[r]])
            wstate["issued"] += 1

    wuse = {"k": 0}

    def w_next(held=0):
        k = wuse["k"]
        wuse["k"] += 1
        w_issue_upto(k + NW - 1 - held)
        r = k % NW
        return wring[r], b_wring[r]

    csems = {"n": 0}

    def cload(tile_ap, src, bufs, sem=None, cnt=None):
        if sem is None:
            sem = new_sem("c%d" % csems["n"]); cnt = [0]
            csems["n"] += 1
        S.dma("sp", tile_ap, src, sem, cnt, writes=bufs)

    cload(xs[0][:, 0:128], ident_d, [b_xs[0]], xs_sem[0], xs_cnt[0])
    cload(gcols[:], gcols_d, [b_gcols])
    S.op("dve", lambda: nc.vector.memset(epsc[:], EPS), writes=[b_epsc])
    S.op("dve", lambda: nc.vector.tensor_copy(out=identb[:], in_=xs[0][:, 0:128]), reads=[b_xs[0]], writes=[b_identb])
    S.op("dve", lambda: nc.vector.memset(onesb[:], 1.0), writes=[b_onesb])
    S.op("dve", lambda: nc.vector.memset(qbT[:], 0.0), writes=b_qbT)
    S.op("dve", lambda: nc.vector.memset(onesf[:], 1.0), writes=[b_onesf])

    def late_setup():
        cload(gfin[:], gfin_d, [b_gfin])
        cload(qkg[:], qkg_d, [b_qkg])
        cload(qkgc[:], qkgc_d, [b_qkgc])
        cload(gsub8[:], gsub_d, [b_gsub8])
        cload(X[:, 1, 0:256], lamv_d, b_X[1])
        cload(X[:, 0, 0:CW], mka_d, b_X[0])
        S.op("dve", lambda: nc.vector.tensor_scalar(out=qkgc[:, 0:1], in0=qkgc[:, 0:1], scalar1=0.125, scalar2=None,
                                                    op0=ALU.mult), reads=[b_qkgc], writes=[b_qkgc])
        S.op("dve", lambda: nc.vector.tensor_scalar(out=gsub8[:], in0=gsub8[:], scalar1=1.0 - LAMBDA_INIT,
                                                    scalar2=None, op0=ALU.mult), reads=[b_gsub8], writes=[b_gsub8])
        lv = X[:, 1, :]
        S.op("dve", lambda: nc.vector.tensor_tensor(out=lv[:, 256:320], in0=lv[:, 0:64], in1=lv[:, 64:128],
                                                    op=ALU.mult), reads=b_X[1], writes=b_X[1])
        S.op("dve", lambda: nc.vector.tensor_tensor(out=lv[:, 320:384], in0=lv[:, 128:192], in1=lv[:, 192:256],
                                                    op=ALU.mult), reads=b_X[1], writes=b_X[1])
        S.op("dve", lambda: nc.vector.tensor_reduce(out=st8[:, 0, 0:2],
                                                    in_=lv[:, 256:384].rearrange("p (a d) -> p a d", a=2),
                                                    axis=AX.X, op=ALU.add), reads=b_X[1], writes=[b_st8])
        S.op("act", lambda: nc.scalar.activation(out=st8[:, 1, 0:2], in_=st8[:, 0, 0:2], func=AF.Exp),
             reads=[b_st8], writes=[b_st8])
        S.op("dve", lambda: nc.vector.scalar_tensor_tensor(out=neglam[:], in0=st8[:, 1, 1:2], scalar=-LAMBDA_INIT,
                                                           in1=st8[:, 1, 0:1], op0=ALU.add, op1=ALU.subtract),
             reads=[b_st8], writes=[b_neglam])
        for h in range(8):
            k = h % 2
            cload(xs[k][:, 0:CW], bg_d[:, h, :], [b_xs[k]], xs_sem[k], xs_cnt[k])
            S.op("dve", lambda h=h, k=k: nc.vector.tensor_tensor(out=Ctab[:, h, :], in0=xs[k][:, 0:CW],
                                                                 in1=X[:, 0, 0:CW], op=ALU.add),
                 reads=[b_xs[k]] + b_X[0], writes=[b_Ctab])

    psrr = {"i": 0}

    def ps_next():
        i = psrr["i"]
        psrr["i"] = (i + 1) % 8
        return i

    identb_ap = identb[:]

    def norm_A(s, src, src_bufs, k):
        S.op("act", lambda: nc.scalar.activation(out=xnb[k][:], in_=src, func=AF.Square,
                                                 accum_out=st4[:, 0, s:s + 1]),
             reads=src_bufs, writes=[b_xnb[k], b_st4[s]])
        S.op("act", lambda: nc.scalar.activation(out=st4[:, 1, s:s + 1], in_=st4[:, 0, s:s + 1], func=AF.Ln,
                                                 scale=1.0 / D, bias=epsc[:]),
             reads=[b_st4[s], b_epsc], writes=[b_st4[s]])
        S.op("act", lambda: nc.scalar.activation(out=st4[:, 2, s:s + 1], in_=st4[:, 1, s:s + 1], func=AF.Exp,
                                                 scale=-0.5),
             reads=[b_st4[s]], writes=[b_st4[s]])
        S.op("dve", lambda: nc.vector.tensor_scalar(out=xnb[k][:], in0=src, scalar1=st4[:, 2, s:s + 1],
                                                    scalar2=None, op0=ALU.mult),
             reads=list(src_bufs) + [b_st4[s]], writes=[b_xnb[k]])

    def norm_B(s, k, gi):
        pi = ps_next()
        pb = ps[pi][:].bitcast(BF16)
        for kc in range(NKC):
            S.op("pe", lambda: nc.tensor.transpose(pb[:, kc * 128:(kc + 1) * 128], xnb[k][:, kc * 128:(kc + 1) * 128],
                                                   identb_ap),
                 reads=[b_xnb[k], b_identb], writes=[b_ps[pi]], inc=(kc == NKC - 1))
        gsl = gcols[:, gi * 8:gi * 8 + 8].unsqueeze(2).broadcast_to([128, 8, 128])
        S.op("dve", lambda: nc.vector.tensor_tensor(out=hT[:, :, s * 128:(s + 1) * 128],
                                                    in0=pb.rearrange("p (c t) -> p c t", c=8), in1=gsl, op=ALU.mult),
             reads=[b_ps[pi], b_gcols], writes=b_hT)

    def norm_to_hT(gi):
        for s in range(NS + 1):
            if s < NS:
                norm_A(s, X[:, s, :], b_X[s], s % 2)
            if s >= 1:
                norm_B(s - 1, (s - 1) % 2, gi)

    def prenorm_load(t, s):
        k = s % 2
        r0 = t * NT + s * 128
        S.dma("sp", xs[k][:], x_d[r0:r0 + 128, :], xs_sem[k], xs_cnt[k], writes=[b_xs[k]])

    def prenorm_A(t, s):
        norm_A(s, xs[s % 2][:], [b_xs[s % 2]], s % 2)

    def ffn_gu(hooks=None):
        for j in range(NFC):
            if hooks is not None and j in hooks:
                hooks[j]()
            wt, wb = w_next()
            wv = wt[:].rearrange("p (k c) -> p k c", k=8)
            pg = ps_next()
            pu = ps_next()
            for kc in range(NKC):
                S.op("pe", lambda: nc.tensor.matmul(ps[pg][:], lhsT=wv[:, kc, 0:128], rhs=hT[:, kc, :],
                                                    start=(kc == 0), stop=(kc == NKC - 1)),
                     reads=[wb, b_hT[kc]], writes=[b_ps[pg]], inc=(kc == NKC - 1))
            for kc in range(NKC):
                S.op("pe", lambda: nc.tensor.matmul(ps[pu][:], lhsT=wv[:, kc, 128:256], rhs=hT[:, kc, :],
                                                    start=(kc == 0), stop=(kc == NKC - 1)),
                     reads=[wb, b_hT[kc]], writes=[b_ps[pu]], inc=(kc == NKC - 1))
            k = j % 2
            S.op("act", lambda: nc.scalar.activation(out=sgt[k][:], in_=ps[pg][:], func=AF.Silu),
                 reads=[b_ps[pg]], writes=[b_sgt[k]])
            S.op("dve", lambda: nc.vector.tensor_tensor(out=ACTB[:, j, :], in0=ps[pu][:], in1=sgt[k][:], op=ALU.mult),
                 reads=[b_ps[pu], b_sgt[k]], writes=[b_ACTB[j]])

    def ffn_down(n, mid=None):
        pd = [ps_next() for _ in range(NS)]
        for g in range(6):
            if mid is not None and g == 3:
                mid()
            wt, wb = w_next()
            wv = wt[:].rearrange("p (k c) -> p k c", k=4)
            nk = 4 if g < 5 else 2
            for k in range(nk):
                kc = g * 4 + k
                for s in range(NS):
                    S.op("pe", lambda: nc.tensor.matmul(ps[pd[s]][:], lhsT=ACTB[:, kc, s * 128:(s + 1) * 128],
                                                        rhs=wv[:, k, :], start=(kc == 0), stop=(kc == NFC - 1)),
                         reads=[wb, b_ACTB[kc]], writes=[b_ps[pd[s]]],
                         inc=(kc == NFC - 1) or (k == nk - 1 and s == NS - 1))
        for s in range(NS):
            xsl = X[:, s, n * 512:(n + 1) * 512]
            S.op("dve", lambda: nc.vector.scalar_tensor_tensor(out=xsl, in0=ps[pd[s]][:], scalar=0.5, in1=xsl,
                                                               op0=ALU.mult, op1=ALU.add),
                 reads=[b_ps[pd[s]], b_X[s][n]], writes=[b_X[s][n]])

    st8x = [sb("st8x%d" % i, [128, 3, 8]) for i in range(2)]; b_st8x = [Buf("st8x0"), Buf("st8x1")]
    sqt2 = [G[2], G[3]]; b_sqt2 = [b_G[2], b_G[3]]
    rt0 = [sb("rt0_%d" % i, [128, 8, 16]) for i in range(2)]; b_rt0 = [Buf("rt0_0"), Buf("rt0_1")]

    def qk_stage1(pi, j, gain_idx, rotary, ksub):
        S.op("act", lambda: nc.scalar.activation(out=sqt2[j][:], in_=ps[pi][:], func=AF.Square),
             reads=[b_ps[pi]], writes=[b_sqt2[j]])
        S.op("dve", lambda: nc.vector.tensor_reduce(out=st8x[j][:, 0, :],
                                                    in_=sqt2[j][:].rearrange("p (h d) -> p h d", h=8),
                                                    axis=AX.X, op=ALU.add), reads=[b_sqt2[j]], writes=[b_st8x[j]])
        S.op("act", lambda: nc.scalar.activation(out=st8x[j][:, 1, :], in_=st8x[j][:, 0, :], func=AF.Ln,
                                                 scale=1.0 / 64, bias=epsc[:]),
             reads=[b_st8x[j], b_epsc], writes=[b_st8x[j]])
        S.op("act", lambda: nc.scalar.activation(out=st8x[j][:, 2, :], in_=st8x[j][:, 1, :], func=AF.Exp, scale=-0.5),
             reads=[b_st8x[j]], writes=[b_st8x[j]])
        if not rotary:
            return
        pv = ps[pi][:].rearrange("p (h d) -> p h d", h=8)
        r16, t1, t2 = rt0[j], rt1[j], rt2[j]
        gv = qkg[:, (gain_idx - 2) * 16:(gain_idx - 2) * 16 + 16].unsqueeze(1).broadcast_to([128, 8, 16])
        S.op("dve", lambda: nc.vector.tensor_tensor(out=r16[:], in0=pv[:, :, 0:16], in1=gv, op=ALU.mult),
             reads=[b_ps[pi], b_qkg], writes=[b_rt0[j]])
        cosv = cs[:, ksub, 0:16].unsqueeze(1).broadcast_to([128, 8, 16])
        sinA = cs[:, ksub, 16:24].unsqueeze(1).broadcast_to([128, 8, 8])
        sinB = cs[:, ksub, 24:32].unsqueeze(1).broadcast_to([128, 8, 8])
        S.op("dve", lambda: nc.vector.tensor_tensor(out=t1[:], in0=r16[:], in1=cosv, op=ALU.mult),
             reads=[b_rt0[j], b_cs], writes=[b_rt1[j]])
        S.op("dve", lambda: nc.vector.tensor_tensor(out=t2[:, :, 0:8], in0=r16[:, :, 8:16], in1=sinA, op=ALU.mult),
             reads=[b_rt0[j], b_cs], writes=[b_rt2[j]])
        S.op("dve", lambda: nc.vector.tensor_tensor(out=t2[:, :, 8:16], in0=r16[:, :, 0:8], in1=sinB, op=ALU.mult),
             reads=[b_rt0[j], b_cs], writes=[b_rt2[j]])
        S.op("dve", lambda: nc.vector.tensor_tensor(out=r16[:], in0=t1[:], in1=t2[:], op=ALU.add),
             reads=[b_rt1[j], b_rt2[j]], writes=[b_rt0[j]])

    def qk_stage2(pi, j, kq, rotary):
        rb = st8x[j][:, 2, :].unsqueeze(2).broadcast_to([128, 8, 64])
        pv = ps[pi][:].rearrange("p (h d) -> p h d", h=8)
        qo = qnb[kq][:].rearrange("p (h d) -> p h d", h=8)
        S.op("dve", lambda: nc.vector.tensor_tensor(out=qo, in0=pv, in1=rb, op=ALU.mult),
             reads=[b_ps[pi], b_st8x[j]], writes=[b_qnb[kq]])
        if rotary:
            S.op("dve", lambda: nc.vector.tensor_tensor(out=qo[:, :, 0:16], in0=rt0[j][:],
                                                        in1=st8x[j][:, 2, :].unsqueeze(2).broadcast_to([128, 8, 16]),
                                                        op=ALU.mult),
                 reads=[b_rt0[j], b_st8x[j]], writes=[b_qnb[kq]])

    def qk_tail(tb, kqs, gain_idx, rotary, dstT, dst_cols, dst_bufs):
        pb = ps[tb][:].bitcast(BF16)
        for i, kq in enumerate(kqs):
            for c in range(4):
                S.op("pe", lambda: nc.tensor.transpose(pb[:, i * 512 + c * 128:i * 512 + (c + 1) * 128],
                                                       qnb[kq][:, c * 128:(c + 1) * 128], identb_ap),
                     reads=[b_qnb[kq], b_identb], writes=[b_ps[tb]], inc=(i == 1 and c == 3))
        for i, kq in enumerate(kqs):
            src = pb[:, i * 512:(i + 1) * 512].rearrange("p (c t) -> p c t", c=4)
            gcol = gain_idx + 2 if rotary else gain_idx
            if dstT is qbT:
                dv = qbT[:].rearrange("p (h e) t -> p h e t", e=2)
                for e in range(2):
                    r0 = 64 * e
                    S.op("act", lambda: nc.scalar.activation(out=dv[r0:r0 + 64, :, e, dst_cols[i]:dst_cols[i] + 128],
                                                             in_=src[r0:r0 + 64], func=AF.Copy,
                                                             scale=qkgc[r0:r0 + 64, gcol:gcol + 1]),
                         reads=[b_ps[tb], b_qkgc], writes=dst_bufs)
                continue
            dst = dstT[:, :, dst_cols[i]:dst_cols[i] + 128]
            S.op("act", lambda: nc.scalar.activation(out=dst, in_=src, func=AF.Copy,
                                                     scale=qkgc[:, gcol:gcol + 1]),
                 reads=[b_ps[tb], b_qkgc], writes=dst_bufs)

    W_ORDER = (0, 1, 3, 2, 4, 5)
    MM_PAIRS = ((0, 1), (2, 3), (4, 5))

    def win_proj(t):
        q0 = t * NT
        ring = t % 2
        pending = []
        u = 0
        for g in W_ORDER:
            wA, bA = w_next()
            wB, bB = w_next(held=1)
            wvs = [wA[:].rearrange("p (k c) -> p k c", k=4), wB[:].rearrange("p (k c) -> p k c", k=4)]
            wbs = [bA, bB]
            for sp in range(2):
                banks = MM_PAIRS[u % 3]
                subs = (2 * sp, 2 * sp + 1)
                for kc in range(NKC):
                    half, k = divmod(kc, 4)
                    for i, s in enumerate(subs):
                        S.op("pe", lambda: nc.tensor.matmul(ps[banks[i]][:], lhsT=hT[:, kc, s * 128:(s + 1) * 128],
                                                            rhs=wvs[half][:, k, :], start=(kc == 0), stop=(kc == NKC - 1)),
                             reads=[wbs[half], b_hT[kc]], writes=[b_ps[banks[i]]],
                             inc=(kc == NKC - 1) or (sp == 1 and k == 3 and i == 1))
                while pending and pending[0][0] <= u - 2:
                    pending.pop(0)[1]()
                if g in (0, 1, 3, 4):
                    rotary = g in (3, 4)
                    gain_idx = {0: 0, 1: 1, 3: 2, 4: 3}[g]
                    kqs = ((2 * u) % 6, (2 * u + 1) % 6)
                    for i, s in enumerate(subs):
                        qk_stage1(banks[i], i, gain_idx, rotary, s)
                    for i, s in enumerate(subs):
                        qk_stage2(banks[i], i, kqs[i], rotary)
                    if g == 0:
                        dstT, cols, dbufs = qaT, [s * 128 for s in subs], b_qaT
                    elif g == 1:
                        dstT, cols, dbufs = KaT, [ring * NT + s * 128 for s in subs], [b_KaT[ring]]
                    elif g == 3:
                        dstT, cols, dbufs = qbT, [s * 128 for s in subs], b_qbT
                    else:
                        dstT, cols, dbufs = KbT, [q0 + s * 128 for s in subs], [b_KbT[t]]
                    tb = 6 + (u % 2)
                    pending.append((u, lambda tb=tb, kqs=kqs, gain_idx=gain_idx, rotary=rotary, dstT=dstT, cols=cols,
                                    dbufs=dbufs: qk_tail(tb, kqs, gain_idx, rotary, dstT, cols, dbufs)))
                else:
                    for i, s in enumerate(subs):
                        ktg = t * NS + s
                        if g == 2:
                            S.op("dve", lambda: nc.vector.tensor_copy(out=Va[:, ring * 4 + s, :], in_=ps[banks[i]][:]),
                                 reads=[b_ps[banks[i]]], writes=[b_Va[ring]])
                        else:
                            S.op("act", lambda: nc.scalar.activation(out=Vb[:, ktg, :], in_=ps[banks[i]][:], func=AF.Copy),
                                 reads=[b_ps[banks[i]]], writes=[b_Vb[t]])
                u += 1
        for i in range(8):
            if i in (1, 3) and pending:
                pending.pop(0)[1]()
            wt, wb = w_next()
            wv = wt[:].rearrange("p (k c) -> p k c", k=8)
            for cc in range(2):
                c = 2 * i + cc
                pg = ps_next()
                for kc in range(NKC):
                    S.op("pe", lambda: nc.tensor.matmul(ps[pg][:], lhsT=wv[:, kc, cc * 128:(cc + 1) * 128],
                                                        rhs=hT[:, kc, :], start=(kc == 0), stop=(kc == NKC - 1)),
                         reads=[wb, b_hT[kc]], writes=[b_ps[pg]], inc=(kc == NKC - 1))
                S.op("act", lambda: nc.scalar.activation(out=ACTB[:, c, :], in_=ps[pg][:], func=AF.Sigmoid),
                     reads=[b_ps[pg]], writes=[b_ACTB[c]])
        while pending:
            pending.pop(0)[1]()

    LAG = 2
    S_BANKS = [0, 1, 2, 3]
    E_CH = [16, 17, 18, 19, 20, 21]

    def attention(t):
        q0 = t * NT
        items = []
        nkt = q0 // 128 + 4
        for c in range(8):
            for kt in range(nkt):
                jd = kt - q0 // 128
                qa = 128 * jd if jd > 0 else 0
                items.append(dict(kind="B", grp=c, sub=0, h=c // 2, e=c % 2, kt=kt, diag=(jd >= 0), qa=qa, qb=512,
                                  first=(kt == 0), last=(kt == nkt - 1), gfirst=(kt == 0), glast=(kt == nkt - 1)))
        itemsA = []
        for hp in range(4):
            js = [j for j in (4, 3, 5, 2, 6, 1, 7, 0) if q0 - 512 + 128 * j >= 0]
            for idx, j in enumerate(js):
                k0 = q0 - 512 + 128 * j
                u0 = 512 - 128 * j
                qa = max(0, -u0)
                qb = min(512, CW - u0)
                itemsA.append(dict(kind="A2", grp=hp, k0=k0, u0=u0, qa=qa, qb=qb,
                                   first=(idx == 0), last=(idx == len(js) - 1),
                                   gfirst=(idx == 0), glast=(idx == len(js) - 1)))
        st = {"sb": 0, "eb": 0, "acc": 0, "step": 0}
        deferred = []
        DEFER = max(3, min(2 * (q0 // 128 + 4) - 4, 12))
        acc_banks = [(4, 5), (6, 7)]

        def emit_S(it):
            pi = S_BANKS[st["sb"] % len(S_BANKS)]
            st["sb"] += 1
            ei = E_CH[st["eb"] % len(E_CH)]
            st["eb"] += 1
            it["pi"] = pi
            it["ei"] = ei
            qa, qb = it["qa"], it["qb"]
            if it["kind"] == "A":
                hp, e, h = it["grp"], it["sub"], it["h"]
                r0 = 64 * e
                k0 = it["k0"]
                tk = k0 // NT
                col = (tk % 2) * NT + (k0 % NT)
                S.op("pe", lambda: nc.tensor.matmul(ps[pi][:, qa:qb], lhsT=KaT[r0:r0 + 64, hp, col:col + 128],
                                                    rhs=qaT[r0:r0 + 64, hp, qa:qb], start=True, stop=False),
                     reads=[b_KaT[tk % 2], b_qaT[hp]], writes=[b_ps[pi]], inc=False)
                u0 = it["u0"]
                S.op("pe", lambda: nc.tensor.matmul(ps[pi][:, qa:qb], lhsT=identb[:], rhs=Ctab[:, h, u0 + qa:u0 + qb],
                                                    start=False, stop=True),
                     reads=[b_identb, b_Ctab], writes=[b_ps[pi]], inc=True)
            else:
                h, e, kt = it["h"], it["e"], it["kt"]
                c = 2 * h + e
                S.op("pe", lambda: nc.tensor.matmul(ps[pi][:, qa:qb], lhsT=KbT[:, h, kt * 128:(kt + 1) * 128],
                                                    rhs=qbT[:, c, qa:qb], start=True, stop=True),
                     reads=[b_KbT[kt // NS], b_qbT[h]], writes=[b_ps[pi]], inc=True)
            S.op("act", lambda: nc.scalar.activation(out=ACTB[:, ei, 0:qb - qa], in_=ps[pi][:, qa:qb], func=AF.Exp,
                                                     scale=(1.0 if it["kind"] == "A" else 0.125)),
                 reads=[b_ps[pi]], writes=[b_ACTB[ei]])
            if it["kind"] == "B" and it["diag"]:
                S.op("act", lambda: nc.scalar.activation(out=ACTB[64:128, ei, 0:64], in_=ACTB[64:128, ei, 0:64],
                                                         func=AF.Copy, scale=0.0),
                     reads=[b_ACTB[ei]], writes=[b_ACTB[ei]])

        def emit_PV(it):
            ei = it["ei"]
            qa, qb = it["qa"], it["qb"]
            if it["gfirst"]:
                st["cur"] = acc_banks[st["acc"] % 2]
                st["acc"] += 1
            po, pl = st["cur"]
            if it["kind"] == "A":
                e, h = it["sub"], it["h"]
                r0 = 64 * e
                k0 = it["k0"]
                tk = k0 // NT
                vi = (tk % 2) * 4 + (k0 % NT) // 128
                S.op("pe", lambda: nc.tensor.matmul(ps[po][r0:r0 + 64, qa:qb], lhsT=Va[:, vi, h * 64:(h + 1) * 64],
                                                    rhs=ACTB[:, ei, 0:qb - qa], start=it["first"], stop=it["last"]),
                     reads=[b_Va[tk % 2], b_ACTB[ei]], writes=[b_ps[po]], inc=False)
                S.op("pe", lambda: nc.tensor.matmul(ps[pl][r0:r0 + 64, qa:qb], lhsT=onesb[:, 0:64],
                                                    rhs=ACTB[:, ei, 0:qb - qa], start=it["first"], stop=it["last"]),
                     reads=[b_onesb, b_ACTB[ei]], writes=[b_ps[pl]], inc=True)
            else:
                h, kt = it["h"], it["kt"]
                S.op("pe", lambda: nc.tensor.matmul(ps[po][:, qa:qb], lhsT=Vb[:, kt, h * 128:(h + 1) * 128],
                                                    rhs=ACTB[:, ei, 0:qb - qa], start=it["first"], stop=it["last"]),
                     reads=[b_Vb[kt // NS], b_ACTB[ei]], writes=[b_ps[po]], inc=False)
                S.op("pe", lambda: nc.tensor.matmul(ps[pl][:, qa:qb], lhsT=onesb[:], rhs=ACTB[:, ei, 0:qb - qa],
                                                    start=it["first"], stop=it["last"]),
                     reads=[b_onesb, b_ACTB[ei]], writes=[b_ps[pl]], inc=True)
            if not it["glast"]:
                return
            if it["kind"] == "A":
                hp = it["grp"]
                k = hp % 2
                S.op("dve", lambda: nc.vector.reciprocal(out=rl[k][:], in_=ps[pl][:]), reads=[b_ps[pl]], writes=[b_rl[k]])
                S.op("dve", lambda: nc.vector.tensor_tensor(out=oT[:, hp, :], in0=ps[po][:], in1=rl[k][:], op=ALU.mult),
                     reads=[b_ps[po], b_rl[k]], writes=[b_oT[hp]])
            else:
                c = it["grp"]
                h, e = c // 2, c % 2
                k = e
                if e == 1:
                    while deferred:
                        deferred.pop(0)[1]()
                S.op("dve", lambda: nc.vector.reciprocal(out=rl[k][:], in_=ps[pl][:]), reads=[b_ps[pl]], writes=[b_rl[k]])
                S.op("dve", lambda: nc.vector.tensor_tensor(out=o12[e][:], in0=ps[po][:], in1=rl[k][:], op=ALU.mult),
                     reads=[b_ps[po], b_rl[k]], writes=[b_o12[e]])
                if e == 1:
                    S.op("dve", lambda: nc.vector.scalar_tensor_tensor(out=sqt[:], in0=o12[1][:], scalar=neglam[:],
                                                                       in1=o12[0][:], op0=ALU.mult, op1=ALU.add),
                         reads=[b_o12[0], b_o12[1], b_neglam], writes=[b_sqt])

                    hold = {}

                    def tail1(h=h):
                        S.op("dve", lambda: nc.vector.tensor_tensor(out=o12[1][:], in0=sqt[:], in1=sqt[:], op=ALU.mult),
                             reads=[b_sqt], writes=[b_o12[1]])
                        pr = S_BANKS[st["sb"] % len(S_BANKS)]
                        st["sb"] += 1
                        hold["pr"] = pr
                        S.op("pe", lambda: nc.tensor.matmul(ps[pr][:], lhsT=onesf[:], rhs=o12[1][:], start=True, stop=True),
                             reads=[b_onesf, b_o12[1]], writes=[b_ps[pr]], inc=True)

                    def tail2(h=h):
                        pr = hold["pr"]
                        S.op("act", lambda: nc.scalar.activation(out=rl[0][:], in_=ps[pr][:], func=AF.Ln,
                                                                 scale=1.0 / 128, bias=epsc[:]),
                             reads=[b_ps[pr], b_epsc], writes=[b_rl[0]])

                    def tail3(h=h):
                        S.op("act", lambda: nc.scalar.activation(out=rl[1][:], in_=rl[0][:], func=AF.Exp, scale=-0.5),
                             reads=[b_rl[0]], writes=[b_rl[1]])
                        S.op("dve", lambda: nc.vector.scalar_tensor_tensor(out=oT[:, 4 + h, :], in0=sqt[:],
                                                                           scalar=gsub8[:], in1=rl[1][:],
                                                                           op0=ALU.mult, op1=ALU.mult),
                             reads=[b_sqt, b_gsub8, b_rl[1]], writes=[b_oT[4 + h]])
                    deferred.append([st["step"] + DEFER, tail1])
                    deferred.append([st["step"] + DEFER + 1, lambda: (tail2(), tail3())])

        def emit_S2(it):
            hp = it["grp"]
            qa, qb = it["qa"], it["qb"]
            k0 = it["k0"]
            u0 = it["u0"]
            tk = k0 // NT
            col = (tk % 2) * NT + (k0 % NT)
            pis, eis = [], []
            for e in range(2):
                pis.append(S_BANKS[st["sb"] % len(S_BANKS)]); st["sb"] += 1
                eis.append(E_CH[st["eb"] % len(E_CH)]); st["eb"] += 1
            it["pis"], it["eis"] = pis, eis
            for e in range(2):
                r0 = 64 * e
                S.op("pe", lambda: nc.tensor.matmul(ps[pis[e]][:, qa:qb], lhsT=KaT[r0:r0 + 64, hp, col:col + 128],
                                                    rhs=qaT[r0:r0 + 64, hp, qa:qb], start=True, stop=False),
                     reads=[b_KaT[tk % 2], b_qaT[hp]], writes=[b_ps[pis[e]]], inc=False)
            for e in range(2):
                h = 2 * hp + e
                S.op("pe", lambda: nc.tensor.matmul(ps[pis[e]][:, qa:qb], lhsT=identb[:],
                                                    rhs=Ctab[:, h, u0 + qa:u0 + qb], start=False, stop=True),
                     reads=[b_identb, b_Ctab], writes=[b_ps[pis[e]]], inc=True)
            for e in range(2):
                S.op("act", lambda: nc.scalar.activation(out=ACTB[:, eis[e], 0:qb - qa], in_=ps[pis[e]][:, qa:qb],
                                                         func=AF.Exp),
                     reads=[b_ps[pis[e]]], writes=[b_ACTB[eis[e]]])

        def emit_PV2(it):
            hp = it["grp"]
            qa, qb = it["qa"], it["qb"]
            eis = it["eis"]
            if it["gfirst"]:
                st["cur"] = acc_banks[st["acc"] % 2]
                st["acc"] += 1
            po, pl = st["cur"]
            k0 = it["k0"]
            tk = k0 // NT
            vi = (tk % 2) * 4 + (k0 % NT) // 128
            for e in range(2):
                h = 2 * hp + e
                r0 = 64 * e
                S.op("pe", lambda: nc.tensor.matmul(ps[po][r0:r0 + 64, qa:qb], lhsT=Va[:, vi, h * 64:(h + 1) * 64],
                                                    rhs=ACTB[:, eis[e], 0:qb - qa], start=it["first"], stop=it["last"]),
                     reads=[b_Va[tk % 2], b_ACTB[eis[e]]], writes=[b_ps[po]], inc=False)
            for e in range(2):
                r0 = 64 * e
                S.op("pe", lambda: nc.tensor.matmul(ps[pl][r0:r0 + 64, qa:qb], lhsT=onesb[:, 0:64],
                                                    rhs=ACTB[:, eis[e], 0:qb - qa], start=it["first"], stop=it["last"]),
                     reads=[b_onesb, b_ACTB[eis[e]]], writes=[b_ps[pl]], inc=(e == 1))
            if it["glast"]:
                k = hp % 2
                S.op("dve", lambda: nc.vector.reciprocal(out=rl[k][:], in_=ps[pl][:]), reads=[b_ps[pl]], writes=[b_rl[k]])
                S.op("dve", lambda: nc.vector.tensor_tensor(out=oT[:, hp, :], in0=ps[po][:], in1=rl[k][:], op=ALU.mult),
                     reads=[b_ps[po], b_rl[k]], writes=[b_oT[hp]])

        def run_deferred(i):
            while deferred and deferred[0][0] <= i:
                deferred.pop(0)[1]()

        n = len(items)
        nA = len(itemsA)
        ia = 0
        pa = 0
        step = 0
        for i in range(n + LAG):
            st["step"] = step
            if i < n:
                emit_S(items[i])
            elif i == n + LAG - 1 and ia < nA:
                emit_S2(itemsA[ia]); ia += 1
            if i - LAG >= 0:
                emit_PV(items[i - LAG])
            run_deferred(step)
            step += 1
        while pa < nA:
            st["step"] = step
            if ia < nA:
                emit_S2(itemsA[ia]); ia += 1
            emit_PV2(itemsA[pa]); pa += 1
            run_deferred(step)
            step += 1
        while deferred:
            deferred.pop(0)[1]()

    def merge():
        for half in range(2):
            wta, wba = w_next()
            wtb, wbb = w_next(held=1)
            wva = wta[:].rearrange("p (k c) -> p k c", k=4)
            wvb = wtb[:].rearrange("p (k c) -> p k c", k=4)
            for cc in range(4):
                c = half * 4 + cc
                pa = ps_next()
                pb = ps_next()
                for kc in range(4):
                    S.op("pe", lambda: nc.tensor.matmul(ps[pb][:], lhsT=wvb[:, kc, cc * 128:(cc + 1) * 128],
                                                        rhs=oT[:, 4 + kc, :], start=(kc == 0), stop=(kc == 3)),
                         reads=[wbb, b_oT[4 + kc]], writes=[b_ps[pb]], inc=(kc == 3))
                for kc in range(4):
                    S.op("pe", lambda: nc.tensor.matmul(ps[pa][:], lhsT=wva[:, kc, cc * 128:(cc + 1) * 128],
                                                        rhs=oT[:, kc, :], start=(kc == 0), stop=(kc == 3)),
                         reads=[wba, b_oT[kc]], writes=[b_ps[pa]], inc=(kc == 3))
                S.op("dve", lambda: nc.vector.tensor_tensor(out=o12[0][:], in0=ps[pa][:], in1=ACTB[:, c, :], op=ALU.mult),
                     reads=[b_ps[pa], b_ACTB[c]], writes=[b_o12[0]])
                S.op("dve", lambda: nc.vector.tensor_tensor(out=o12[1][:], in0=ps[pb][:], in1=ACTB[:, 8 + c, :],
                                                            op=ALU.mult),
                     reads=[b_ps[pb], b_ACTB[8 + c]], writes=[b_o12[1]])
                S.op("dve", lambda: nc.vector.tensor_tensor(out=hT[:, c, :], in0=o12[0][:], in1=o12[1][:], op=ALU.add),
                     reads=[b_o12[0], b_o12[1]], writes=[b_hT[c]])
        for n in range(2):
            pw = [ps_next() for _ in range(NS)]
            for half in range(2):
                wt, wb = w_next()
                wv = wt[:].rearrange("p (k c) -> p k c", k=4)
                for k in range(4):
                    kc = half * 4 + k
                    for s in range(NS):
                        S.op("pe", lambda: nc.tensor.matmul(ps[pw[s]][:], lhsT=hT[:, kc, s * 128:(s + 1) * 128],
                                                            rhs=wv[:, k, :], start=(kc == 0), stop=(kc == NKC - 1)),
                             reads=[wb, b_hT[kc]], writes=[b_ps[pw[s]]],
                             inc=(kc == NKC - 1) or (k == 3 and s == NS - 1))
            for s in range(NS):
                xs = X[:, s, n * 512:(n + 1) * 512]
                S.op("dve", lambda: nc.vector.tensor_tensor(out=xs, in0=ps[pw[s]][:], in1=xs, op=ALU.add),
                     reads=[b_ps[pw[s]], b_X[s][n]], writes=[b_X[s][n]])

    def final_sq(s):
        bx = b_X[s]
        k = s % 2
        S.op("act", lambda: nc.scalar.activation(out=xs[k][:], in_=X[:, s, :], func=AF.Square,
                                                 accum_out=st4[:, 0, s:s + 1]),
             reads=bx, writes=[b_xs[k], b_st4[s]])

    def final_rstd():
        S.op("act", lambda: nc.scalar.activation(out=st4[:, 1, :], in_=st4[:, 0, :], func=AF.Ln,
                                                 scale=1.0 / D, bias=epsc[:]),
             reads=b_st4 + [b_epsc], writes=b_st4)
        S.op("act", lambda: nc.scalar.activation(out=st4[:, 2, :], in_=st4[:, 1, :], func=AF.Exp, scale=-0.5),
             reads=b_st4, writes=b_st4)

    def final_store(t, s):
        q0 = t * NT
        k = s % 2
        S.op("dve", lambda: nc.vector.scalar_tensor_tensor(out=xs[k][:], in0=X[:, s, :], scalar=st4[:, 2, s:s + 1],
                                                           in1=gfin[:], op0=ALU.mult, op1=ALU.mult),
             reads=b_X[s] + [b_st4[s], b_gfin], writes=[b_xs[k]])
        return S.dma("sp", out_d[q0 + s * 128:q0 + (s + 1) * 128, :], xs[k][:], xs_sem[k], xs_cnt[k],
                     reads=[b_xs[k]])

    def final_hooks(t):
        def x_load():
            q1 = (t + 1) * NT
            S.dma("sp", X[:], x_d[q1:q1 + NT, :].rearrange("(s p) c -> p s c", p=128), x_sem, x_cnt, writes=allX)
        return {2: lambda: final_sq(0), 3: lambda: final_sq(1), 4: lambda: (final_rstd_pair(0), final_store(t, 0), final_store(t, 1)),
                5: lambda: final_sq(2), 6: lambda: final_sq(3), 7: lambda: (final_rstd_pair(2), final_store(t, 2), final_store(t, 3)),
                8: x_load}

    def final_rstd_pair(s0):
        S.op("act", lambda: nc.scalar.activation(out=st4[:, 1, s0:s0 + 2], in_=st4[:, 0, s0:s0 + 2], func=AF.Ln,
                                                 scale=1.0 / D, bias=epsc[:]),
             reads=[b_st4[s0], b_st4[s0 + 1], b_epsc], writes=[b_st4[s0], b_st4[s0 + 1]])
        S.op("act", lambda: nc.scalar.activation(out=st4[:, 2, s0:s0 + 2], in_=st4[:, 1, s0:s0 + 2], func=AF.Exp,
                                                 scale=-0.5),
             reads=[b_st4[s0], b_st4[s0 + 1]], writes=[b_st4[s0], b_st4[s0 + 1]])

    last_toks = []
    allX = [b for bs in b_X for b in bs]
    prenorm_load(0, 0); prenorm_load(0, 1)
    prenorm_A(0, 0); prenorm_A(0, 1)
    norm_B(0, 0, 0)
    prenorm_load(0, 2); prenorm_A(0, 2)
    norm_B(1, 1, 0)
    prenorm_load(0, 3); prenorm_A(0, 3)
    norm_B(2, 0, 0)
    norm_B(3, 1, 0)
    late_setup()
    for t in range(n_tiles):
        q0 = t * NT
        if t == 0:
            S.dma("sp", X[:], x_d[q0:q0 + NT, :].rearrange("(s p) c -> p s c", p=128), x_sem, x_cnt, writes=allX)
        if t == 0:
            dump("d_h0T", hT[:].rearrange("p a b -> p (a b)"), b_hT, [128, NKC * NT], BF16)
        S.mark('F1gu')
        ffn_gu(hooks=(final_hooks(t - 1) if t > 0 else None))
        S.mark('F1dn')
        ffn_down(0)
        ffn_down(1)
        S.mark('N2')
        if t == 0:
            dump("d_x1", X[:].rearrange("p a b -> p (a b)"), allX, [128, NS * D])
        norm_to_hT(1)
        S.mark('W')
        S.dma("sp", cs[:].rearrange("p a b -> p (a b)"), cs_d[:, t * 128:(t + 1) * 128], cs_sem, cs_cnt, writes=[b_cs])
        win_proj(t)
        S.mark('A')
        if t == 0:
            dump("d_qaT", qaT[:].rearrange("p a b -> p (a b)"), b_qaT, [128, 4 * NT], BF16)
            dump("d_KaT", KaT[:].rearrange("p a b -> p (a b)"), b_KaT, [128, 8 * NT], BF16)
            dump("d_KbT", KbT[:, :, 0:NT], [b_KbT[0]], [128, 4, NT], BF16)
            dump("d_Va", Va[:, 0:4, :], b_Va, [128, 4, 512], BF16)
            dump("d_Vb", Vb[:, 0:4, :], [b_Vb[0]], [128, 4, 512], BF16)
            dump("d_gate", ACTB[:, 0:16, :], b_ACTB[0:16], [128, 16, NT], BF16)
        attention(t)
        if t == 0:
            dump("d_oT", oT[:].rearrange("p a b -> p (a b)"), b_oT, [128, 8 * NT], BF16)
        S.mark('M')
        merge()
        S.mark('N3')
        if t == 0:
            dump("d_yT", hT[:].rearrange("p a b -> p (a b)"), b_hT, [128, NKC * NT], BF16)
            dump("d_x2", X[:].rearrange("p a b -> p (a b)"), allX, [128, NS * D])
        norm_to_hT(2)
        S.mark('F2gu')
        nxt = t + 1 < n_tiles
        ffn_gu()
        S.mark('F2dn')
        if nxt:
            prenorm_load(t + 1, 0); prenorm_load(t + 1, 1)
            prenorm_A(t + 1, 0); prenorm_A(t + 1, 1)
        ffn_down(0)
        if nxt:
            norm_B(0, 0, 0); norm_B(1, 1, 0)
            prenorm_load(t + 1, 2); prenorm_load(t + 1, 3)
            prenorm_A(t + 1, 2); prenorm_A(t + 1, 3)
        ffn_down(1, mid=((lambda: (norm_B(2, 0, 0), norm_B(3, 1, 0))) if nxt else None))
        if t == 0:
            dump("d_x3", X[:].rearrange("p a b -> p (a b)"), allX, [128, NS * D])
        S.mark('N4')
        if not nxt:
            for s_ in range(NS):
                final_sq(s_)
            final_rstd()
            last_toks = [final_store(t, s_) for s_ in range(NS)]
    for tok in last_toks:
        S.wait_tok("sp", tok)
    assert wuse["k"] == n_tiles * NSLOT, wuse["k"]
    nc._phase_marks = S.marks
    return nc


def _pack_weights(inp):
    slots = []

    def panel(W, cols_list):
        parts = [W[:, c0:c1].reshape(8, 128, c1 - c0).transpose(1, 0, 2) for (c0, c1) in cols_list]
        return np.concatenate(parts, axis=2).reshape(128, 2048)

    def rows(W, kc0, nk, c0):
        a = np.zeros((128, 4, 512), np.float32)
        a[:, :nk, :] = W[kc0 * 128:(kc0 + nk) * 128, c0:c0 + 512].reshape(nk, 128, 512).transpose(1, 0, 2)
        return a.reshape(128, 2048)

    def ffn_slots(wgu, wdn):
        for j in range(NFC):
            slots.append(panel(wgu, [(j * 128, (j + 1) * 128), (FF + j * 128, FF + (j + 1) * 128)]))
        for n in range(2):
            for g in range(6):
                slots.append(rows(wdn, g * 4, 4 if g < 5 else 2, n * 512))

    ffn_slots(inp["w_ffn1_gu"][0], inp["w_ffn1_down"][0])
    win = inp["w_in"][0]
    for g in (0, 1, 3, 2, 4, 5):
        for half in range(2):
            slots.append(rows(win, half * 4, 4, g * 512))
    for i in range(8):
        slots.append(panel(win, [(3072 + i * 256, 3072 + (i + 1) * 256)]))
    wa, wb, wo = inp["w_up_a"][0], inp["w_up_b"][0], inp["w_out"][0]
    for half in range(2):
        slots.append(rows(wa, 0, 4, half * 512))
        slots.append(rows(wb, 0, 4, half * 512))
    for n in range(2):
        for half in range(2):
            slots.append(rows(wo, half * 4, 4, n * 512))
    ffn_slots(inp["w_ffn2_gu"][0], inp["w_ffn2_down"][0])
    assert len(slots) == NSLOT
    return np.ascontiguousarray(np.stack(slots, 0))


def _consts(inp):
    f = np.float32
    c = {}
    c["ident"] = np.eye(128, dtype=f)
    g = np.stack([inp["g_ffn1"][0], inp["g_mix"][0], inp["g_ffn2"][0]], 0)
    c["gcols"] = np.ascontiguousarray(g.reshape(3, 8, 128).transpose(2, 0, 1).reshape(128, 24)).astype(f)
    c["gfin"] = np.ascontiguousarray(np.broadcast_to(inp["g_final"][0][None, :], (128, D))).astype(f)
    qk = np.concatenate([inp["qn_a"][0], inp["kn_a"][0], inp["qn_b"][0], inp["kn_b"][0]])
    qk16 = np.concatenate([inp["qn_b"][0][:16], inp["kn_b"][0][:16]])
    c["qkg"] = np.ascontiguousarray(np.broadcast_to(qk16[None, :], (128, 32))).astype(f)
    qk4 = qk.reshape(4, 64)
    dsel = (np.arange(64) < 16)[None, :]
    qk6 = np.concatenate([qk4, np.where(dsel, np.float32(1.0), qk4[2:4])], 0)
    c["qkgc"] = np.ascontiguousarray(np.tile(qk6.T, (2, 1))).astype(f)
    lv = np.concatenate([inp["lambda_q1"][0], inp["lambda_k1"][0], inp["lambda_q2"][0], inp["lambda_k2"][0]])
    c["lamv"] = np.ascontiguousarray(np.broadcast_to(lv[None, :], (128, 256))).astype(f)
    c["gsub"] = np.ascontiguousarray(inp["g_subln"][0].reshape(128, 1)).astype(f)
    ki = np.arange(128)[:, None]
    u = np.arange(CW)[None, :]
    idx = np.clip(u - ki, -256, 256) + 256
    rel = inp["rel_bias"][0]
    c["bg"] = np.ascontiguousarray(rel[idx].transpose(0, 2, 1)).astype(f)
    dch = (u // 64) - (ki // 64)
    c["mka"] = np.where((dch >= 0) & (dch <= 8), 0.0, NEG).astype(f)
    pos = np.arange(SEQ, dtype=f)
    inv = (np.float32(500000.0) ** (-np.arange(0, 16, 2, dtype=f) / np.float32(16))).astype(f)
    ang = (pos[:, None] * inv[None, :]).astype(f)
    cosv, sinv = np.cos(ang).astype(f), np.sin(ang).astype(f)
    tab = np.concatenate([cosv, cosv, -sinv, sinv], axis=1)
    c["cs"] = np.ascontiguousarray(tab.reshape(32, 128, 32).transpose(1, 0, 2).reshape(128, 32 * 32)).astype(f)
    return c


_NC_CACHE = {}


def kernel(**inputs):
    inp = {k: np.asarray(v, dtype=np.float32) for k, v in inputs.items()}
    x = inp["x"]
    B = x.shape[0]
    if "nc" not in _NC_CACHE:
        _NC_CACHE["nc"] = build()
    nc = _NC_CACHE["nc"]
    shared = _consts(inp)
    shared["wpack"] = _pack_weights(inp)
    in_maps = []
    for b in range(B):
        m = dict(shared)
        m["x"] = np.ascontiguousarray(x[b])
        in_maps.append(m)
    res = run_bass_kernel_spmd(nc, in_maps, core_ids=list(range(B)))
    out = np.stack([np.asarray(res.results[b]["out"], dtype=np.float32).reshape(SEQ, D) for b in range(B)], 0)
    return out
```
